# Optimizing a Trainium2 kernel written in Bass

```python
import jax, jax.numpy as jnp
from jax import lax
import numpy as np

D_MODEL = 1024
BATCH = 16
SEQ = 2048
DEPTH = 2

HEAD_DIM = 64
FOX_HEADS = D_MODEL // HEAD_DIM
SWA_Q_HEADS = D_MODEL // HEAD_DIM
SWA_KV_HEADS = max(1, SWA_Q_HEADS // 8)
SWA_GROUP = SWA_Q_HEADS // SWA_KV_HEADS
WINDOW = 128
Q_BLOCK = 128
D_FF = ((8 * D_MODEL // 3 + 127) // 128) * 128
N_MIXERS = 2
RMS_EPS = 1e-6
NEG_INF = -1e30

kernel_name = "fox_swa_sink_alibi_macaron_hybrid"


def rmsnorm(x, g):
    xf = x.astype(jnp.float32)
    y = xf * lax.rsqrt(jnp.mean(xf * xf, axis=-1, keepdims=True) + RMS_EPS)
    return (y * g.astype(jnp.float32)).astype(x.dtype)


def swiglu(h, w_gate, w_up, w_down):
    return (jax.nn.silu(h @ w_gate) * (h @ w_up)) @ w_down


def alibi_slopes(n_heads):
    return jnp.asarray(2.0 ** (-8.0 * np.arange(1, n_heads + 1) / n_heads), dtype=jnp.float32)


def fox_mixer(h, w_in, b_forget, w_out):
    B, S, D = h.shape
    H, hd = FOX_HEADS, HEAD_DIM
    proj = h @ w_in
    q, k, v, f_logit = jnp.split(proj, [H * hd, 2 * H * hd, 3 * H * hd], axis=-1)
    q = q.reshape(B, S, H, hd)
    k = k.reshape(B, S, H, hd)
    v = v.reshape(B, S, H, hd)
    log_f = jax.nn.log_sigmoid((f_logit + b_forget).astype(jnp.float32))
    c = jnp.cumsum(log_f, axis=1).transpose(0, 2, 1)
    scale = hd ** -0.5
    n_blocks = S // Q_BLOCK
    outs = []
    for i in range(n_blocks):
        q0, q1 = i * Q_BLOCK, (i + 1) * Q_BLOCK
        qb, kb, vb = q[:, q0:q1], k[:, :q1], v[:, :q1]
        s = jnp.einsum('bqhd,bkhd->bhqk', qb, kb).astype(jnp.float32) * scale
        s = s + (c[:, :, q0:q1, None] - c[:, :, None, :q1])
        causal = (q0 + jnp.arange(Q_BLOCK))[:, None] >= jnp.arange(q1)[None, :]
        s = jnp.where(causal, s, NEG_INF)
        p = jax.nn.softmax(s, axis=-1).astype(vb.dtype)
        outs.append(jnp.einsum('bhqk,bkhd->bqhd', p, vb))
    o = jnp.concatenate(outs, axis=1).reshape(B, S, H * hd)
    return o @ w_out


def swa_sink_mixer(h, w_in, sinks, w_out):
    B, S, D = h.shape
    Hq, Hkv, G, hd, W = SWA_Q_HEADS, SWA_KV_HEADS, SWA_GROUP, HEAD_DIM, WINDOW
    proj = h @ w_in
    q, k, v = jnp.split(proj, [Hq * hd, Hq * hd + Hkv * hd], axis=-1)
    nb = S // W
    q = q.reshape(B, nb, W, Hkv, G, hd)
    k = k.reshape(B, nb, W, Hkv, hd)
    v = v.reshape(B, nb, W, Hkv, hd)
    pad = jnp.zeros_like(k[:, :1])
    k_band = jnp.concatenate([jnp.concatenate([pad, k[:, :-1]], axis=1), k], axis=2)
    v_band = jnp.concatenate([jnp.concatenate([pad, v[:, :-1]], axis=1), v], axis=2)
    s = jnp.einsum('bnqkgd,bnskd->bnkgqs', q, k_band).astype(jnp.float32) * (hd ** -0.5)
    qi = jnp.arange(W)[:, None]
    kj = jnp.arange(2 * W)[None, :]
    dist = qi + W - kj
    valid = (dist >= 0) & (dist < W)
    blk = jnp.arange(nb)[:, None, None]
    valid = valid[None] & ((blk > 0) | (kj[None] >= W))
    slopes = alibi_slopes(Hq).reshape(Hkv, G)
    s = s - slopes[:, :, None, None] * dist.astype(jnp.float32)[None, None]
    s = jnp.where(valid[None, :, None, None], s, NEG_INF)
    sink = jnp.broadcast_to(sinks.astype(jnp.float32).reshape(1, 1, Hkv, G, 1, 1), s.shape[:-1] + (1,))
    p = jax.nn.softmax(jnp.concatenate([s, sink], axis=-1), axis=-1)[..., :-1]
    o = jnp.einsum('bnkgqs,bnskd->bnqkgd', p.astype(v_band.dtype), v_band).reshape(B, S, Hq * hd)
    return o @ w_out


def setup_inputs(seed: int = 0) -> dict:
    key = jax.random.key(seed)
    ks = iter(jax.random.split(key, 64))
    f32 = jnp.float32

    def nrm(shape, scale):
        return jax.random.normal(next(ks), shape, f32) * scale

    def gain():
        return 1.0 + nrm((D_MODEL,), 0.01)

    def ffn():
        return (gain(), nrm((D_MODEL, D_FF), D_MODEL ** -0.5), nrm((D_MODEL, D_FF), D_MODEL ** -0.5),
                nrm((D_FF, D_MODEL), D_FF ** -0.5))

    inp = {"x": nrm((BATCH, SEQ, D_MODEL), 1.0)}
    g, wg, wu, wd = ffn()
    inp.update(l0_ffn1_norm=g, l0_ffn1_w_gate=wg, l0_ffn1_w_up=wu, l0_ffn1_w_down=wd)
    inp["l0_mix_norm"] = gain()
    w_qkv = nrm((D_MODEL, 3 * FOX_HEADS * HEAD_DIM), D_MODEL ** -0.5)
    w_f = nrm((D_MODEL, FOX_HEADS), 0.1 * D_MODEL ** -0.5)
    inp["l0_fox_w_in"] = jnp.concatenate([w_qkv, w_f], axis=1)
    inp["l0_fox_b_forget"] = 3.0 + nrm((FOX_HEADS,), 0.5)
    inp["l0_fox_w_out"] = nrm((FOX_HEADS * HEAD_DIM, D_MODEL), (FOX_HEADS * HEAD_DIM) ** -0.5)
    g, wg, wu, wd = ffn()
    inp.update(l0_ffn2_norm=g, l0_ffn2_w_gate=wg, l0_ffn2_w_up=wu, l0_ffn2_w_down=wd)
    g, wg, wu, wd = ffn()
    inp.update(l1_ffn1_norm=g, l1_ffn1_w_gate=wg, l1_ffn1_w_up=wu, l1_ffn1_w_down=wd)
    inp["l1_mix_norm"] = gain()
    inp["l1_swa_w_in"] = nrm((D_MODEL, (SWA_Q_HEADS + 2 * SWA_KV_HEADS) * HEAD_DIM), D_MODEL ** -0.5)
    inp["l1_swa_sinks"] = nrm((SWA_Q_HEADS,), 0.5)
    inp["l1_swa_w_out"] = nrm((SWA_Q_HEADS * HEAD_DIM, D_MODEL), (SWA_Q_HEADS * HEAD_DIM) ** -0.5)
    g, wg, wu, wd = ffn()
    inp.update(l1_ffn2_norm=g, l1_ffn2_w_gate=wg, l1_ffn2_w_up=wu, l1_ffn2_w_down=wd)
    inp["final_norm"] = gain()
    return inp


def reference(x,
              l0_ffn1_norm, l0_ffn1_w_gate, l0_ffn1_w_up, l0_ffn1_w_down,
              l0_mix_norm, l0_fox_w_in, l0_fox_b_forget, l0_fox_w_out,
              l0_ffn2_norm, l0_ffn2_w_gate, l0_ffn2_w_up, l0_ffn2_w_down,
              l1_ffn1_norm, l1_ffn1_w_gate, l1_ffn1_w_up, l1_ffn1_w_down,
              l1_mix_norm, l1_swa_w_in, l1_swa_sinks, l1_swa_w_out,
              l1_ffn2_norm, l1_ffn2_w_gate, l1_ffn2_w_up, l1_ffn2_w_down,
              final_norm):
    layers = [
        ((l0_ffn1_norm, l0_ffn1_w_gate, l0_ffn1_w_up, l0_ffn1_w_down),
         (l0_mix_norm, l0_fox_w_in, l0_fox_b_forget, l0_fox_w_out),
         (l0_ffn2_norm, l0_ffn2_w_gate, l0_ffn2_w_up, l0_ffn2_w_down)),
        ((l1_ffn1_norm, l1_ffn1_w_gate, l1_ffn1_w_up, l1_ffn1_w_down),
         (l1_mix_norm, l1_swa_w_in, l1_swa_sinks, l1_swa_w_out),
         (l1_ffn2_norm, l1_ffn2_w_gate, l1_ffn2_w_up, l1_ffn2_w_down)),
    ]
    mixers = (fox_mixer, swa_sink_mixer)
    h = x
    for i in range(DEPTH):
        ffn1, mix, ffn2 = layers[i]
        h = h + 0.5 * swiglu(rmsnorm(h, ffn1[0]), *ffn1[1:])
        h = h + mixers[i % N_MIXERS](rmsnorm(h, mix[0]), *mix[1:])
        h = h + 0.5 * swiglu(rmsnorm(h, ffn2[0]), *ffn2[1:])
    return rmsnorm(h, final_norm)
```

```python
import numpy as np
from contextlib import ExitStack
import concourse.bass as bass
import concourse.mybir as mybir
from concourse.bass_utils import run_bass_kernel_spmd

F32 = mybir.dt.float32
BF16 = mybir.dt.bfloat16
AF = mybir.ActivationFunctionType
ALU = mybir.AluOpType

D = 1024
S = 2048
DFF = 2816
NCH = 8
TT = 512
NTT = S // TT
NJ = S // 128
NH = 16
G = 2
NGRP = DFF // (128 * G)
NBUF = 3
REG = 2048
SLOT = 3 * REG
EPS = 1e-6
ALL_PHASES = ["l0_ffn1", "l0_fox", "l0_ffn2", "l1_ffn1", "l1_swa", "l1_ffn2", "final"]
NORM_IDX = {"l0_ffn1": 0, "l0_fox": 1, "l0_ffn2": 2, "l1_ffn1": 3, "l1_swa": 4, "l1_ffn2": 5, "final": 6}
SLOPES = [float(2.0 ** (-8.0 * (h + 1) / NH)) for h in range(NH)]


class T:
    __slots__ = ("name", "lw", "rd", "dsem", "dkey", "dcnt")

    def __init__(self, name):
        self.name = name
        self.lw = None
        self.rd = {}
        self.dsem = None
        self.dkey = None
        self.dcnt = 0


class KB:
    def __init__(self, nc, es):
        self.nc = nc
        self.es = es
        self.eng = {"pe": nc.tensor, "act": nc.scalar, "dve": nc.vector, "pool": nc.gpsimd, "sp": nc.sync}
        self.sem = {k: es.enter_context(nc.semaphore("s_" + k)) for k in self.eng}
        self.cnt = {k: 0 for k in self.eng}
        self.waited = {k: {} for k in self.eng}
        self.nsem = 0

    def _need(self, e, reads, writes, skipkey=None):
        need = {}

        def add(st):
            if st is None:
                return
            k, sem, v = st
            if k == skipkey:
                return
            if e == "pe" and k == "pe":
                return
            if k not in need or need[k][1] < v:
                need[k] = (sem, v)

        for t in reads:
            add(t.lw)
        for t in writes:
            add(t.lw)
            for st in t.rd.values():
                add(st)
        w = self.waited[e]
        for k, (sem, v) in need.items():
            if w.get(k, 0) < v:
                self.eng[e].wait_ge(sem, v)
                w[k] = v

    def op(self, e, fn, reads=(), writes=(), inc=True):
        self._need(e, reads, writes)
        ins = fn(self.eng[e])
        if inc:
            self.cnt[e] += 1
            ins.then_inc(self.sem[e], 1)
            st = (e, self.sem[e], self.cnt[e])
        else:
            st = (e, self.sem[e], self.cnt[e] + 1)
        for t in writes:
            t.lw = st
            t.rd = {}
        for t in reads:
            t.rd[e] = st

    def fence(self):
        for e in ("pe", "act", "dve", "sp", "pool"):
            for o in ("pe", "act", "dve", "pool"):
                if e == o or self.cnt[o] == 0:
                    continue
                if self.waited[e].get(o, 0) < self.cnt[o]:
                    self.eng[e].wait_ge(self.sem[o], self.cnt[o])
                    self.waited[e][o] = self.cnt[o]

    def dma(self, q, out, in_, reads=(), writes=()):
        t = writes[0]
        if t.dsem is None:
            t.dkey = "d%d" % self.nsem
            t.dsem = self.es.enter_context(self.nc.semaphore(t.dkey))
            self.nsem += 1
        self._need(q, reads, writes, skipkey=t.dkey)
        ins = self.eng[q].dma_start(out=out, in_=in_)
        t.dcnt += 16
        ins.then_inc(t.dsem, 16)
        st = (t.dkey, t.dsem, t.dcnt)
        for w in writes:
            w.lw = st
            w.rd = {}
        for r in reads:
            r.rd[t.dkey] = st


def bcast(ap, pos, count):
    dims = [list(d) for d in ap.ap]
    dims.insert(1 + pos, [0, count])
    return bass.AP(ap.tensor, ap.offset, dims)


def weight_tasks(phases, n_seq):
    tasks = []
    for s in range(n_seq):
        for ph in phases:
            if ph.endswith("ffn1") or ph.endswith("ffn2"):
                for g in range(NGRP):
                    tasks.append(("ffn", ph, g))
            elif ph == "l0_fox":
                tasks.append(("fox_f", ph, 0))
                for hp in range(8):
                    tasks.append(("fox_pair", ph, hp))
            elif ph == "l1_swa":
                tasks.append(("swa_kv", ph, 0))
                for tg in range(NTT):
                    for hp in range(8):
                        tasks.append(("swa_q", ph, hp))
    return tasks


def build_program(phases, n_seq):
    nc = bass.Bass("TRN2", target_bir_lowering=False)
    es = ExitStack()
    k = KB(nc, es)
    dr = {}

    def din(name, shape):
        dr[name] = nc.dram_tensor(name, list(shape), F32, kind="ExternalInput").ap()
        return dr[name]

    xT = din("xT", [n_seq, D, S])
    gains = din("gains", [128, 7 * NCH])
    c_ones = din("c_ones", [128, 128])
    c_U = din("c_U", [128, 128])
    c_tri = din("c_tri", [128, 128])
    c_ident = din("c_ident", [128, 128])
    c_amask = din("c_amask", [128, 128])
    for ph in phases:
        if "ffn" in ph:
            din(ph + "_wg", [D, DFF])
            din(ph + "_wu", [D, DFF])
            din(ph + "_wd", [DFF, D])
        elif ph == "l0_fox":
            din("fox_win", [D, 3 * D + NH])
            din("fox_bfb", [128, NH])
            din("fox_bcol", [NH, 1])
            din("fox_wout", [D, D])
        elif ph == "l1_swa":
            din("swa_win", [D, D + 256])
            din("swa_sinkb", [128, NH])
            din("swa_wout", [D, D])
            din("c_distc", [128, 256])
            din("c_valid", [128, 256])
    yT = nc.dram_tensor("yT", [n_seq, D, S], F32, kind="ExternalOutput").ap()
    y_t = T("yT")

    def sb(name, shape, dt):
        return es.enter_context(nc.sbuf_tensor(name, list(shape), dt))

    x_sb = sb("x_sb", [128, NCH, S], F32)
    h_sb = sb("h_sb", [128, NCH, S], BF16)
    g_sb = sb("g_sb", [128, 7 * NCH], F32)
    ones_sb = sb("ones_sb", [128, 128], F32)
    U_sb = sb("U_sb", [128, 128], F32)
    tri_sb = sb("tri_sb", [128, 128], BF16)
    onesb_sb = sb("onesb_sb", [128, 128], BF16)
    ident_sb = sb("ident_sb", [128, 128], BF16)
    amask_sb = sb("amask_sb", [128, 128], BF16)
    ring = sb("ring", [128, NBUF * SLOT], BF16)
    ARENA = 72704
    arena = sb("arena", [128, ARENA // 4], F32)
    banks = [es.enter_context(nc.psum_tensor("P%d" % i, [128, 512], F32)) for i in range(8)]
    bank_t = [T("P%d" % i) for i in range(8)]

    def av(off, nbytes, dt):
        assert off % 4 == 0 and off + nbytes <= ARENA
        a = arena[:, off // 4:(off + nbytes) // 4]
        if dt == BF16:
            a = a.bitcast(BF16) if hasattr(a, "bitcast") else None
        return a

    x_t = [[T("x%d_%d" % (c, tt)) for tt in range(NTT)] for c in range(NCH)]
    h_t = [T("h%d" % tt) for tt in range(NTT)]
    g_t = T("g")
    const_t = T("consts")
    tri_t = T("tri")
    onesb_t = T("onesb")
    cmask_t = T("cmask")

    k.dma("sp", g_sb[:], gains, writes=[g_t])
    k.dma("sp", ones_sb[:], c_ones, writes=[const_t])
    k.dma("sp", U_sb[:], c_U, writes=[const_t])
    k.dma("pool", tri_sb[:], c_tri, writes=[tri_t])
    k.dma("pool", onesb_sb[:], c_ones, writes=[onesb_t])
    k.dma("pool", ident_sb[:], c_ident, writes=[cmask_t])
    k.dma("pool", amask_sb[:], c_amask, writes=[cmask_t])

    tasks = weight_tasks(phases, n_seq)
    reg_t = [[T("r%d_%d" % (s_, r)) for r in range(3)] for s_ in range(NBUF)]
    wstate = {"loaded": 0, "cur": -1}

    def rview(slot, r, off, n):
        base = slot * SLOT + r * REG + off
        return ring[:, base:base + n]

    def emit_load(i):
        kind, ph, idx = tasks[i]
        slot = i % NBUF
        rt = reg_t[slot]
        if kind == "ffn":
            f0 = idx * G * 128
            wg = dr[ph + "_wg"].rearrange("(c p) n -> p c n", p=128)[:, :, f0:f0 + G * 128]
            wu = dr[ph + "_wu"].rearrange("(c p) n -> p c n", p=128)[:, :, f0:f0 + G * 128]
            wd = dr[ph + "_wd"].rearrange("(g p) n -> p g n", p=128)[:, idx * G:(idx + 1) * G, :]
            k.dma("pool", rview(slot, 0, 0, REG).rearrange("p (c n) -> p c n", c=NCH), wg, writes=[rt[0]])
            k.dma("pool", rview(slot, 1, 0, REG).rearrange("p (c n) -> p c n", c=NCH), wu, writes=[rt[1]])
            k.dma("pool", rview(slot, 2, 0, REG).rearrange("p (g n) -> p g n", g=G), wd, writes=[rt[2]])
        elif kind == "fox_f":
            w = dr["fox_win"].rearrange("(c p) n -> p c n", p=128)[:, :, 3 * D:3 * D + NH]
            k.dma("pool", rview(slot, 0, 0, NCH * NH).rearrange("p (c n) -> p c n", c=NCH), w, writes=[rt[0]])
        elif kind == "fox_pair":
            w = dr["fox_win"].rearrange("(c p) n -> p c n", p=128)
            c0 = idx * 128
            k.dma("pool", rview(slot, 0, 0, 1024).rearrange("p (c n) -> p c n", c=NCH), w[:, :, c0:c0 + 128], writes=[rt[0]])
            k.dma("pool", rview(slot, 0, 1024, 1024).rearrange("p (c n) -> p c n", c=NCH), w[:, :, D + c0:D + c0 + 128], writes=[rt[0]])
            k.dma("pool", rview(slot, 1, 0, 1024).rearrange("p (c n) -> p c n", c=NCH), w[:, :, 2 * D + c0:2 * D + c0 + 128], writes=[rt[1]])
            k.dma("pool", rview(slot, 1, 1024, 1024), dr["fox_wout"][c0:c0 + 128, :], writes=[rt[1]])
        elif kind == "swa_kv":
            w = dr["swa_win"].rearrange("(c p) n -> p c n", p=128)
            dst = rview(slot, 0, 0, REG).rearrange("p (c k n) -> p c k n", c=NCH, k=2)
            for kvh in range(2):
                for dup in range(2):
                    k.dma("pool", dst[:, :, kvh, dup * 64:dup * 64 + 64], w[:, :, D + kvh * 64:D + kvh * 64 + 64], writes=[rt[0]])
            k.dma("pool", rview(slot, 1, 0, 1024).rearrange("p (c n) -> p c n", c=NCH), w[:, :, D + 128:D + 256], writes=[rt[1]])
        elif kind == "swa_q":
            w = dr["swa_win"].rearrange("(c p) n -> p c n", p=128)
            c0 = idx * 128
            k.dma("pool", rview(slot, 0, 0, 1024).rearrange("p (c n) -> p c n", c=NCH), w[:, :, c0:c0 + 128], writes=[rt[0]])

    def wnext(kind):
        wstate["cur"] += 1
        i = wstate["cur"]
        assert tasks[i][0] == kind, (tasks[i], kind)
        while wstate["loaded"] <= min(i + 1, len(tasks) - 1):
            emit_load(wstate["loaded"])
            wstate["loaded"] += 1
        return i % NBUF

    def fview(off, n):
        return arena[:, off // 4: off // 4 + n]

    def bview(off, n):
        full = arena[:, off // 4: off // 4 + (n + 1) // 2]
        return full.bitcast(BF16)

    sq8 = bview(0, NCH * TT).rearrange("p (c n) -> p c n", c=NCH)
    sq8_t = T("sq8")
    rs = fview(8192, 512)
    rs_t = T("rs")
    rstd = [fview(18432, 512), fview(20480, 512)]
    rstd_t = [T("rstd0"), T("rstd1")]
    cnt = {"sq": 0, "rstd": 0, "gate": 0, "up": 0, "down": 0, "sg": 0, "act": 0, "proj": 0, "sbank": 0, "obank": 0, "pt": 0, "rd": 0}

    def rot(name, n):
        v = cnt[name] % n
        cnt[name] += 1
        return v

    def tsl(tt):
        return slice(tt * TT, (tt + 1) * TT)

    def emit_norm(nidx, final=False):
        for tt in range(NTT):
            ps, ps_t = banks[7], bank_t[7]
            k.op("act", lambda e: e.activation(out=sq8, in_=x_sb[:, :, tsl(tt)], func=AF.Square),
                 reads=[x_t[c][tt] for c in range(NCH)], writes=[sq8_t])
            for c in range(NCH):
                k.op("pe", lambda e: e.matmul(ps[:], onesb_sb[:], sq8[:, c, :], start=(c == 0), stop=(c == NCH - 1)),
                     reads=[sq8_t, onesb_t], writes=[ps_t], inc=(c == NCH - 1))
            k.op("act", lambda e: e.activation(out=rs, in_=ps[:], func=AF.Ln, scale=1.0 / D, bias=eps_sb[:, 0:1]),
                 reads=[ps_t, eps_t], writes=[rs_t])
            r = rot("rstd", 2)
            k.op("act", lambda e: e.activation(out=rstd[r], in_=rs, func=AF.Exp, scale=-0.5), reads=[rs_t], writes=[rstd_t[r]])
            for c in range(NCH):
                gcol = g_sb[:, nidx * NCH + c: nidx * NCH + c + 1]
                if final:
                    k.op("dve", lambda e: e.scalar_tensor_tensor(out=x_sb[:, c, tsl(tt)], in0=x_sb[:, c, tsl(tt)], scalar=gcol,
                                                                 in1=rstd[r], op0=ALU.mult, op1=ALU.mult),
                         reads=[rstd_t[r], g_t], writes=[x_t[c][tt]])
                else:
                    k.op("dve", lambda e: e.scalar_tensor_tensor(out=h_sb[:, c, tsl(tt)], in0=x_sb[:, c, tsl(tt)], scalar=gcol,
                                                                 in1=rstd[r], op0=ALU.mult, op1=ALU.mult),
                         reads=[x_t[c][tt], rstd_t[r], g_t], writes=[h_t[tt]])

    eps_sb = sb("eps_sb", [128, 1], F32)
    eps_t = T("eps")
    k.op("dve", lambda e: e.memset(eps_sb[:], EPS), writes=[eps_t])

    sg = [fview(10240, 512), fview(12288, 512)]
    sg_t = [T("sg0"), T("sg1")]
    actb = [bview(14336, G * 512).rearrange("p (g n) -> p g n", g=G), bview(14336 + G * 1024, G * 512).rearrange("p (g n) -> p g n", g=G)]
    act_t = [T("act0"), T("act1")]

    def emit_ffn(ph):
        pending = [None]
        for g in range(NGRP):
            slot = wnext("ffn")
            rt = reg_t[slot]
            wg = rview(slot, 0, 0, REG).rearrange("p (c n) -> p c n", c=NCH)
            wu = rview(slot, 1, 0, REG).rearrange("p (c n) -> p c n", c=NCH)
            wd = rview(slot, 2, 0, REG).rearrange("p (g n) -> p g n", g=G)
            for tt in range(NTT):
                a = rot("act", 2)
                for gi in range(G):
                    gb = rot("gate", 2)
                    ub = 2 + rot("up", 2)
                    for c in range(NCH):
                        k.op("pe", lambda e: e.matmul(banks[gb][:], wg[:, c, gi * 128:(gi + 1) * 128], h_sb[:, c, tsl(tt)],
                                                      start=(c == 0), stop=(c == NCH - 1)),
                             reads=[rt[0], h_t[tt]], writes=[bank_t[gb]], inc=(c == NCH - 1))
                    for c in range(NCH):
                        k.op("pe", lambda e: e.matmul(banks[ub][:], wu[:, c, gi * 128:(gi + 1) * 128], h_sb[:, c, tsl(tt)],
                                                      start=(c == 0), stop=(c == NCH - 1)),
                             reads=[rt[1], h_t[tt]], writes=[bank_t[ub]], inc=(c == NCH - 1))
                    si = rot("sg", 2)
                    k.op("act", lambda e: e.activation(out=sg[si], in_=banks[gb][:], func=AF.Silu),
                         reads=[bank_t[gb]], writes=[sg_t[si]])
                    k.op("dve", lambda e: e.tensor_tensor(out=actb[a][:, gi, :], in0=sg[si], in1=banks[ub][:], op=ALU.mult),
                         reads=[sg_t[si], bank_t[ub]], writes=[act_t[a]])
                if pending[0] is not None:
                    pending[0]()

                def down(a=a, tt=tt, wd=wd, rt=rt):
                    for dc in range(NCH):
                        db = 4 + rot("down", 4)
                        for gi in range(G):
                            k.op("pe", lambda e: e.matmul(banks[db][:], wd[:, gi, dc * 128:(dc + 1) * 128], actb[a][:, gi, :],
                                                          start=(gi == 0), stop=(gi == G - 1)),
                                 reads=[rt[2], act_t[a]], writes=[bank_t[db]], inc=(gi == G - 1))
                        k.op("dve", lambda e: e.scalar_tensor_tensor(out=x_sb[:, dc, tsl(tt)], in0=banks[db][:], scalar=0.5,
                                                                     in1=x_sb[:, dc, tsl(tt)], op0=ALU.mult, op1=ALU.add),
                             reads=[bank_t[db]], writes=[x_t[dc][tt]])
                pending[0] = down
        pending[0]()

    one_sb = sb("one_sb", [128, 1], F32)
    k.op("dve", lambda e: e.memset(one_sb[:], 1.0), writes=[eps_t])

    proj_banks = [[0, 1]]

    def projbank():
        pbk = proj_banks[0]
        i = pbk[rot("proj", len(pbk))]
        return banks[i], bank_t[i]

    class Pipe:
        def __init__(self, delay):
            self.q = []
            self.delay = delay

        def push(self, fn):
            self.q.append(fn)
            if len(self.q) > self.delay:
                self.q.pop(0)()

        def flush(self):
            while self.q:
                self.q.pop(0)()

    def emit_outproj(wo, wo_t, oT, oT_t):
        for dc in range(NCH):
            for tg in range(NTT):
                pb, pb_t = projbank()
                k.op("pe", lambda e: e.matmul(pb[:], wo[:, dc * 128:(dc + 1) * 128], oT[:, tsl(tg)], start=True, stop=True),
                     reads=[wo_t, oT_t[tg]], writes=[pb_t])
                k.op("dve", lambda e: e.tensor_tensor(out=x_sb[:, dc, tsl(tg)], in0=pb[:], in1=x_sb[:, dc, tsl(tg)], op=ALU.add),
                     reads=[pb_t], writes=[x_t[dc][tg]])

    def gen_outproj_tg(wo, wo_t, oT, oT_t, tg):
        for dc in range(NCH):
            pb, pb_t = projbank()
            k.op("pe", lambda e: e.matmul(pb[:], wo[:, dc * 128:(dc + 1) * 128], oT[:, tsl(tg)], start=True, stop=True),
                 reads=[wo_t, oT_t[tg]], writes=[pb_t])
            k.op("dve", lambda e: e.tensor_tensor(out=x_sb[:, dc, tsl(tg)], in0=pb[:], in1=x_sb[:, dc, tsl(tg)], op=ALU.add),
                 reads=[pb_t], writes=[x_t[dc][tg]])
            yield

    ucount = [0]

    def bg_step(bg):
        while bg:
            if len(bg[0]) > 2 and bg[0][2] > ucount[0]:
                return
            try:
                next(bg[0][1])
                return
            except StopIteration:
                bg.pop(0)

    def bg_drain(bg):
        while bg:
            drain(bg.pop(0)[1])

    def bg_has_older(bg, hp):
        return any(ent[0] < hp for ent in bg)

    def gen_projT(w, w_t, evac):
        for tg in range(NTT):
            pb, pb_t = projbank()
            for c in range(NCH):
                k.op("pe", lambda e: e.matmul(pb[:], w[:, c, :], h_sb[:, c, tsl(tg)], start=(c == 0), stop=(c == NCH - 1)),
                     reads=[w_t, h_t[tg]], writes=[pb_t], inc=(c == NCH - 1))
            evac(tg, pb, pb_t)
            yield

    def gen_projV(w, w_t, vaug, vaug_t):
        for jg in range(4):
            pb, pb_t = projbank()
            pv = pb[:].rearrange("p (j n) -> p j n", j=4)
            for jj in range(4):
                j = jg * 4 + jj
                for c in range(NCH):
                    k.op("pe", lambda e: e.matmul(pv[:, jj, :], h_sb[:, c, j * 128:(j + 1) * 128], w[:, c, :],
                                                  start=(c == 0), stop=(c == NCH - 1)),
                         reads=[w_t, h_t[j // 4]], writes=[pb_t], inc=(c == NCH - 1))
            k.op("dve", lambda e: e.tensor_copy(out=vaug[:, jg * 4:(jg + 1) * 4, :, 0:64],
                                                in_=pb[:].rearrange("p (j h d) -> p j h d", j=4, h=2)),
                 reads=[pb_t], writes=[vaug_t])
            yield

    def drain(gen):
        if gen is not None:
            for _ in gen:
                pass

    def emit_fox():
        pT = [bview(i * 1024, 512) for i in range(4)] + [bview(8192, 512), bview(9216, 512)]
        rd = [fview(4096, 512), fview(6144, 512)]
        A = fview(8192, 512)
        cpT = [fview(0, 512), fview(2048, 512)]
        qaug = [[bview(10240 + b_ * 16384 + hh_ * 4096, S) for hh_ in range(2)] for b_ in range(2)]
        kaug = [[bview(10240 + b_ * 16384 + 8192 + hh_ * 4096, S) for hh_ in range(2)] for b_ in range(2)]
        vaug = [bview(43008 + i * 8192, NJ * 2 * 128).rearrange("p (j h n) -> p j h n", j=NJ, h=2) for i in range(2)]
        oT = bview(59392, S)
        zel = fview(63488, NJ * NH).rearrange("p (j h) -> p j h", j=NJ)
        totp = fview(64512, (NJ + 1) * NH).rearrange("p (j h) -> p j h", j=NJ + 1)
        cp = fview(66560, NJ * NH).rearrange("p (j h) -> p j h", j=NJ)
        bfb = fview(67584, NH)
        bcol = fview(67648, 1)
        dq = bview(68608, S)
        qa_t = [[[T("qa%d_%d_%d" % (b_, hh_, i)) for i in range(NTT)] for hh_ in range(2)] for b_ in range(2)]
        ka_t = [[[T("ka%d_%d_%d" % (b_, hh_, i)) for i in range(NTT)] for hh_ in range(2)] for b_ in range(2)]
        qrow_t = [[T("qrow%d_%d" % (b_, hh_)) for hh_ in range(2)] for b_ in range(2)]
        augq_t = [[T("augq%d_%d" % (b_, hh_)) for hh_ in range(2)] for b_ in range(2)]
        augk_t = [[T("augk%d_%d" % (b_, hh_)) for hh_ in range(2)] for b_ in range(2)]
        oT_t = [T("oT%d" % i) for i in range(NTT)]
        vaug_t = [T("vaug0"), T("vaug1")]
        zel_t, totp_t, cp_t, bfb_t, A_t, dq_t = T("zel"), T("totp"), T("cp"), T("bfb"), T("A"), T("dq")
        cpT_t = [T("cpT0"), T("cpT1")]
        pT_t = [T("pT%d" % i) for i in range(6)]
        rd_t = [T("rd0"), T("rd1")]

        k.dma("sp", bfb, dr["fox_bfb"], writes=[bfb_t])
        k.dma("sp", bcol[0:NH, :], dr["fox_bcol"], writes=[bfb_t])
        slot = wnext("fox_f")
        wf = rview(slot, 0, 0, NCH * NH).rearrange("p (c n) -> p c n", c=NCH)
        wf_t = reg_t[slot][0]
        identf = fview(67712, NH)
        identf_t = T("identf")
        k.dma("sp", identf[0:NH, :], dr["c_ident"][0:NH, 0:NH], writes=[identf_t])
        cp_ps = banks[6][:, 0:NJ * NH].rearrange("p (j h) -> p j h", j=NJ)
        for tg in range(NTT):
            pb, pb_t = banks[tg], bank_t[tg]
            for c in range(NCH):
                k.op("pe", lambda e: e.matmul(pb[0:NH, :], wf[:, c, :], h_sb[:, c, tsl(tg)], start=(c == 0), stop=(c == NCH - 1)),
                     reads=[wf_t, h_t[tg]], writes=[pb_t], inc=(c == NCH - 1))
        A2 = [A, fview(63488, 512)]
        A2_t = [A_t, T("A1")]
        cpT4 = [fview(i * 2048, 512) for i in range(4)]
        cpT4_t = [T("cpT4_%d" % i) for i in range(4)]
        for tg in range(NTT):
            pb, pb_t = banks[tg], bank_t[tg]
            Aa, Aa_t = A2[tg % 2], A2_t[tg % 2]
            k.op("dve", lambda e: e.tensor_scalar(out=Aa[0:NH, :], in0=pb[0:NH, :], scalar1=bcol[0:NH, 0:1], scalar2=None, op0=ALU.add),
                 reads=[pb_t, bfb_t], writes=[Aa_t])
            k.op("act", lambda e: e.activation(out=Aa[0:NH, :], in_=Aa[0:NH, :], func=AF.Exp, scale=-1.0), reads=[Aa_t], writes=[Aa_t])
            k.op("act", lambda e: e.activation(out=Aa[0:NH, :], in_=Aa[0:NH, :], func=AF.Ln, scale=1.0, bias=one_sb[0:NH, 0:1]),
                 reads=[Aa_t, eps_t], writes=[Aa_t])
            init = 0.0 if tg == 0 else cpT4[tg - 1][0:NH, TT - 1:TT]
            rds = [Aa_t] + ([cpT4_t[tg - 1]] if tg > 0 else [])
            k.op("dve", lambda e: e.tensor_tensor_scan(out=cpT4[tg][0:NH, :], data0=Aa[0:NH, :], data1=Aa[0:NH, :], initial=init,
                                                       op0=ALU.add, op1=ALU.max),
                 reads=rds, writes=[cpT4_t[tg]])
            k.op("dve", lambda e: e.tensor_scalar(out=dq[0:NH, tsl(tg)], in0=cpT4[tg][0:NH, :], scalar1=-8.0, scalar2=None, op0=ALU.mult),
                 reads=[cpT4_t[tg]], writes=[dq_t])
        ONES2 = 1.0019378662109375
        one_b32 = bass.AP(ones_sb[64:128, 0:1].tensor, ones_sb[64:128, 0:1].offset, [list(ones_sb[64:128, 0:1].ap[0]), [0, S // 2]])
        one_row = bass.AP(ones_sb[64:65, 0:1].tensor, ones_sb[64:65, 0:1].offset, [list(ones_sb[64:65, 0:1].ap[0]), [0, S]])
        for b in range(2):
            v32 = fview(43008 + b * 8192, NJ * 2 * 64).rearrange("p (j h n) -> p j h n", j=NJ, h=2)
            k.op("dve", lambda e: e.memset(v32[:, :, :, 32:64], ONES2), writes=[vaug_t[b]])
            for hh in range(2):
                q32 = fview(10240 + b * 16384 + hh * 4096, S // 2)
                k32 = fview(10240 + b * 16384 + 8192 + hh * 4096, S // 2)
                k.op("dve", lambda e: e.memset(q32[64:128, :], 0.0), writes=[augq_t[b][hh]])
                k.op("act", lambda e: e.activation(out=k32[64:128, :], in_=one_b32, func=AF.Copy, scale=0.0),
                     reads=[const_t], writes=[augk_t[b][hh]])
                k.op("act", lambda e: e.activation(out=kaug[b][hh][64:65, :], in_=one_row, func=AF.Copy, scale=1.0),
                     reads=[const_t], writes=[augk_t[b][hh]])

        def emit_cp():
            for tg in range(NTT):
                for jj in range(4):
                    j = tg * 4 + jj
                    k.op("pe", lambda e: e.transpose(out=cp_ps[:, j, :], in_=cpT4[tg][0:NH, jj * 128:(jj + 1) * 128],
                                                     identity=identf[0:NH, :]),
                         reads=[cpT4_t[tg], identf_t], writes=[bank_t[6]])
            k.op("dve", lambda e: e.tensor_copy(out=cp, in_=cp_ps), reads=[bank_t[6]], writes=[cp_t])

        def start_proj(hp):
            slot = wnext("fox_pair")
            rt = reg_t[slot]
            b = hp % 2
            wq = rview(slot, 0, 0, 1024).rearrange("p (c n) -> p c n", c=NCH)
            wk = rview(slot, 0, 1024, 1024).rearrange("p (c n) -> p c n", c=NCH)
            wv = rview(slot, 1, 0, 1024).rearrange("p (c n) -> p c n", c=NCH)
            wo = rview(slot, 1, 1024, 1024)
            for hh in range(2):
                k.dma("sp", qaug[b][hh][64:65, :], dq[2 * hp + hh:2 * hp + hh + 1, :], reads=[dq_t, augq_t[b][hh]], writes=[qrow_t[b][hh]])

            def evac_to(dst, dst_t):
                def evac(tg, pb, pb_t):
                    for hh in range(2):
                        k.op("dve", lambda e: e.tensor_copy(out=dst[hh][0:64, tsl(tg)], in_=pb[hh * 64:hh * 64 + 64, :]),
                             reads=[pb_t], writes=[dst_t[hh][tg]])
                return evac

            def g():
                yield from gen_projT(wq, rt[0], evac_to(qaug[b], qa_t[b]))
                yield from gen_projT(wk, rt[0], evac_to(kaug[b], ka_t[b]))
                yield from gen_projV(wv, rt[1], vaug[b], vaug_t[b])
            return g(), (wo, rt[1])

        gen, wo_info = start_proj(0)
        drain(gen)
        gen = None
        emit_cp()
        proj_banks[0] = [0, 1, 7]
        NUNITS = 80
        NSTEPS = 16
        pipe = Pipe(3)
        bg = []
        for hp in range(8):
            b = hp % 2
            cur_wo = wo_info
            started = False
            credit = 0.0
            uidx = 0
            for hh in range(2):
                h = 2 * hp + hh
                rows = slice(hh * 64, hh * 64 + 64)
                for tg in range(NTT):
                    obi = 4 + rot("obank", 2)
                    ob, ob_t = banks[obi], bank_t[obi]
                    nkb = 4 * (tg + 1)
                    for kb in range(nkb):
                        c0 = max(0, kb * 128 - tg * TT)
                        sbi = (2, 3, 6)[rot("sbank", 3)]
                        sbk, sbk_t = banks[sbi], bank_t[sbi]
                        diag = kb >= 4 * tg
                        k.op("pe", lambda e: e.matmul(sbk[:, c0:TT], kaug[b][hh][:, kb * 128:(kb + 1) * 128],
                                                      qaug[b][hh][:, tg * TT + c0:(tg + 1) * TT], start=True, stop=not diag),
                             reads=[ka_t[b][hh][kb // 4], qa_t[b][hh][tg], qrow_t[b][hh], augq_t[b][hh], augk_t[b][hh]], writes=[sbk_t], inc=not diag)
                        if diag:
                            k.op("pe", lambda e: e.matmul(sbk[:, c0:c0 + 128], ident_sb[:], amask_sb[:], start=False, stop=True),
                                 reads=[cmask_t], writes=[sbk_t])
                        pi = rot("pt", 6)
                        k.op("act", lambda e: e.activation(out=pT[pi][:, c0:TT], in_=sbk[:, c0:TT], func=AF.Exp,
                                                           scale=0.125, bias=cp[:, kb, h:h + 1]),
                             reads=[sbk_t, cp_t], writes=[pT_t[pi]])

                        def pv(ob=ob, ob_t=ob_t, c0=c0, kb=kb, hh=hh, pi=pi, nkb=nkb, tg=tg, rows=rows, b=b, wo=cur_wo, hp=hp):
                            k.op("pe", lambda e: e.matmul(ob[:, c0:TT], vaug[b][:, kb, hh, :], pT[pi][:, c0:TT],
                                                          start=(kb == 0), stop=(kb == nkb - 1)),
                                 reads=[vaug_t[b], pT_t[pi]], writes=[ob_t])
                            if kb == nkb - 1:
                                ri = rot("rd", 2)
                                k.op("act", lambda e: e.activation(out=rd[ri][0:64, :], in_=ob[64:128, :], func=AF.Ln),
                                     reads=[ob_t], writes=[rd_t[ri]])
                                k.op("act", lambda e: e.activation(out=rd[ri][0:64, :], in_=rd[ri][0:64, :], func=AF.Exp, scale=-1.0),
                                     reads=[rd_t[ri]], writes=[rd_t[ri]])
                                k.op("dve", lambda e: e.tensor_tensor(out=oT[rows, tsl(tg)], in0=ob[0:64, :], in1=rd[ri][0:64, :],
                                                                      op=ALU.mult),
                                     reads=[ob_t, rd_t[ri]], writes=[oT_t[tg]])
                                if hh == 1:
                                    bg.append((hp, gen_outproj_tg(wo[0], wo[1], oT, oT_t, tg), ucount[0] + 7))
                        pipe.push(pv)
                        ucount[0] += 1
                        bg_step(bg)
                        uidx += 1
                        if (not started) and hp + 1 < 8 and uidx >= 4 and not bg_has_older(bg, hp):
                            gen, wo_info = start_proj(hp + 1)
                            started = True
                        if gen is not None:
                            credit += NSTEPS / (NUNITS - 14) + 0.02
                            while credit >= 1.0:
                                credit -= 1.0
                                next(gen, None)
            if hp + 1 < 8:
                assert started
            drain(gen)
            gen = None
        pipe.flush()
        bg_drain(bg)
        proj_banks[0] = [0, 1]

    def emit_swa():
        pT = [bview(i * 1024, 512).rearrange("p (s h n) -> p s h n", s=2, h=2) for i in range(3)]
        rd = [fview(4096, 512), fview(6144, 512)]
        ef = [fview(8192, 512).rearrange("p (s h n) -> p s h n", s=2, h=2),
              fview(10240, 512).rearrange("p (s h n) -> p s h n", s=2, h=2),
              fview(12288, 512).rearrange("p (s h n) -> p s h n", s=2, h=2)]
        qbd = [bview(14336 + b_ * 2048, 1024).rearrange("p (n h q) -> p n h q", n=4, h=2) for b_ in range(2)]
        kTd = bview(18432, 2 * S).rearrange("p (k t) -> p k t", k=2)
        vaug = bview(26624, NJ * 2 * 128).rearrange("p (j h n) -> p j h n", j=NJ, h=2)
        oT = bview(34816, 8 * TT).rearrange("p (g t) -> p g t", g=8)
        E = bview(43008, NH * 256).rearrange("p (h s n) -> p h s n", h=NH, s=2)
        wo_all = bview(51200, 8 * D).rearrange("p (g n) -> p g n", g=8)
        distc = fview(67584, 256)
        valid = fview(68608, 256)
        es_ = fview(69632, NH)
        sinkb = fview(69696, NH)
        qT_t = [T("sqT0"), T("sqT1")]
        qz_t = T("qz")
        kT_t = [[T("skT%d_%d" % (kv, i)) for i in range(NTT)] for kv in range(2)]
        oT_t = [T("soT0"), T("soT1")]
        wo_t = T("wo_all")
        vaug_t, E_t, cst_t, es_t = T("svaug"), T("E"), T("scst"), T("es")
        ef_t = [T("ef%d" % i) for i in range(3)]
        pT_t = [T("spT%d" % i) for i in range(3)]
        rd_t = [T("srd0"), T("srd1")]

        k.dma("sp", distc, dr["c_distc"], writes=[cst_t])
        k.dma("sp", valid, dr["c_valid"], writes=[cst_t])
        k.dma("sp", sinkb, dr["swa_sinkb"], writes=[cst_t])
        k.dma("pool", wo_all, dr["swa_wout"].rearrange("(g p) n -> p g n", p=128), writes=[wo_t])
        k.op("dve", lambda e: e.memset(vaug[:, :, :, 64:128], 1.0), writes=[vaug_t])
        for b_ in range(2):
            k.op("dve", lambda e: e.memset(qbd[b_][0:64, :, 1, :], 0.0), writes=[qz_t])
            k.op("dve", lambda e: e.memset(qbd[b_][64:128, :, 0, :], 0.0), writes=[qz_t])
        k.op("act", lambda e: e.activation(out=es_, in_=sinkb, func=AF.Exp), reads=[cst_t], writes=[es_t])
        ef0 = fview(8192, 256)
        ef1 = fview(10240, 256)
        for h in range(NH):
            i = h % 2
            efx = (ef0, ef1)[i]
            k.op("act", lambda e: e.activation(out=efx, in_=distc, func=AF.Exp, scale=-SLOPES[h]),
                 reads=[cst_t], writes=[ef_t[i]])
            k.op("dve", lambda e: e.tensor_tensor(out=E[:, h].rearrange("p s n -> p (s n)"), in0=efx, in1=valid, op=ALU.mult),
                 reads=[ef_t[i], cst_t], writes=[E_t])
        slot = wnext("swa_kv")
        rt = reg_t[slot]
        wkd = rview(slot, 0, 0, REG).rearrange("p (c k n) -> p c k n", c=NCH, k=2)
        wv = rview(slot, 1, 0, 1024).rearrange("p (c n) -> p c n", c=NCH)

        def evac_k(kvh):
            def evac(tg, pb, pb_t):
                k.op("dve", lambda e: e.tensor_copy(out=kTd[:, kvh, tsl(tg)], in_=pb[:]), reads=[pb_t], writes=[kT_t[kvh][tg]])
            return evac
        for kvh in range(2):
            drain(gen_projT(wkd[:, :, kvh, :], rt[0], evac_k(kvh)))
        drain(gen_projV(wv, rt[1], vaug, vaug_t))

        def qproj(tg, b):
            slot = wnext("swa_q")
            w_t = reg_t[slot][0]
            wq = rview(slot, 0, 0, 1024).rearrange("p (c n) -> p c n", c=NCH)
            pb, pb_t = projbank()
            for c in range(NCH):
                k.op("pe", lambda e: e.matmul(pb[:], wq[:, c, :], h_sb[:, c, tsl(tg)], start=(c == 0), stop=(c == NCH - 1)),
                     reads=[w_t, h_t[tg]], writes=[pb_t], inc=(c == NCH - 1))
            for hh in range(2):
                r = slice(hh * 64, hh * 64 + 64)
                k.op("dve", lambda e: e.tensor_copy(out=qbd[b][r, :, hh, :], in_=pb[r, :].rearrange("p (n q) -> p n q", n=4)),
                     reads=[pb_t, qz_t], writes=[qT_t[b]])

        def gen_outproj_half(tg, half):
            for dc in range(NCH):
                pb, pb_t = projbank()
                for g in range(4 * half, 4 * half + 4):
                    k.op("pe", lambda e: e.matmul(pb[:], wo_all[:, g, dc * 128:(dc + 1) * 128], oT[:, g, :],
                                                  start=(g == 4 * half), stop=(g == 4 * half + 3)),
                         reads=[wo_t, oT_t[half]], writes=[pb_t], inc=(g == 4 * half + 3))
                k.op("dve", lambda e: e.tensor_tensor(out=x_sb[:, dc, tsl(tg)], in0=pb[:], in1=x_sb[:, dc, tsl(tg)], op=ALU.add),
                     reads=[pb_t], writes=[x_t[dc][tg]])
                yield

        def bg_drain_upto(bg, tag):
            while bg and bg[0][0] <= tag:
                drain(bg.pop(0)[1])

        pipe = Pipe(2)
        bg = []
        steps = [(tg, hp) for tg in range(NTT) for hp in range(8)]
        proj_banks[0] = [0, 1, 7]
        qproj(0, 0)
        for si, (tg, hp) in enumerate(steps):
            b = si % 2
            kvh = hp // 4
            Eperm = E[:, 2 * hp:2 * hp + 2].rearrange("p h s n -> p s h n")
            obs = []
            for hh in range(2):
                obi = (4, 5, 6)[rot("obank", 3)]
                obs.append((banks[obi], bank_t[obi]))
            for nn in range(4):
                n = 4 * tg + nn
                s0 = 0 if n > 0 else 1
                sbi = 2 + rot("sbank", 2)
                sbk, sbk_t = banks[sbi], bank_t[sbi]
                sv = sbk[:].rearrange("p (s h n) -> p s h n", s=2, h=2)
                for sl in range(s0, 2):
                    kb = n - 1 + sl
                    k.op("pe", lambda e: e.matmul(sv[:, sl], kTd[:, kvh, kb * 128:(kb + 1) * 128], qbd[b][:, nn],
                                                  start=True, stop=True),
                         reads=[kT_t[kvh][kb // 4], qT_t[b], qz_t], writes=[sbk_t], inc=(sl == 1))
                i = rot("pt", 3)
                k.op("act", lambda e: e.activation(out=ef[i][:, s0:2], in_=sv[:, s0:2], func=AF.Exp, scale=0.125),
                     reads=[sbk_t], writes=[ef_t[i]])
                k.op("dve", lambda e: e.tensor_tensor(out=pT[i][:, s0:2], in0=ef[i][:, s0:2], in1=Eperm[:, s0:2], op=ALU.mult),
                     reads=[ef_t[i], E_t], writes=[pT_t[i]])

                def pv(obs=obs, n=n, nn=nn, s0=s0, i=i, tg=tg, hp=hp, kvh=kvh):
                    for hh in range(2):
                        ob, ob_t = obs[hh]
                        for sl in range(s0, 2):
                            kb = n - 1 + sl
                            k.op("pe", lambda e: e.matmul(ob[:, nn * 128:(nn + 1) * 128], vaug[:, kb, kvh, :],
                                                          pT[i][:, sl, hh, :], start=(sl == s0), stop=(sl == 1)),
                                 reads=[vaug_t, pT_t[i]], writes=[ob_t], inc=(sl == 1))
                    if nn == 3:
                        half = hp // 4
                        bg_drain_upto(bg, (tg - 1) * 2 + half)
                        for hh in range(2):
                            ob, ob_t = obs[hh]
                            h = 2 * hp + hh
                            rows = slice(hh * 64, hh * 64 + 64)
                            ri = rot("rd", 2)
                            k.op("act", lambda e: e.activation(out=rd[ri][0:64, :], in_=ob[64:128, :], func=AF.Ln,
                                                               scale=1.0, bias=es_[0:64, h:h + 1]),
                                 reads=[ob_t, es_t], writes=[rd_t[ri]])
                            k.op("act", lambda e: e.activation(out=rd[ri][0:64, :], in_=rd[ri][0:64, :], func=AF.Exp, scale=-1.0),
                                 reads=[rd_t[ri]], writes=[rd_t[ri]])
                            k.op("dve", lambda e: e.tensor_tensor(out=oT[rows, hp, :], in0=ob[0:64, :], in1=rd[ri][0:64, :],
                                                                  op=ALU.mult),
                                 reads=[ob_t, rd_t[ri]], writes=[oT_t[half]])
                        if hp % 4 == 3:
                            bg.append((tg * 2 + half, gen_outproj_half(tg, half), ucount[0] + 3))
                pipe.push(pv)
                ucount[0] += 1
                bg_step(bg)
                if nn == 1 and si + 1 < len(steps):
                    qproj(steps[si + 1][0], 1 - b)
        pipe.flush()
        bg_drain(bg)
        proj_banks[0] = [0, 1]

    for s in range(n_seq):
        xv = xT[s].rearrange("(c p) t -> p c t", p=128)
        for tt in range(NTT):
            k.dma("sp", x_sb[:, :, tsl(tt)], xv[:, :, tsl(tt)], writes=[x_t[c][tt] for c in range(NCH)])
        prev_ph = None
        for ph in phases:
            if ph == "final":
                emit_norm(NORM_IDX[ph], final=True)
            else:
                if "ffn" in ph:
                    if not (prev_ph is not None and "ffn" in prev_ph):
                        k.fence()
                    emit_norm(NORM_IDX[ph])
                else:
                    emit_norm(NORM_IDX[ph])
                    k.fence()
                if "ffn" in ph:
                    emit_ffn(ph)
                elif ph == "l0_fox":
                    emit_fox()
                elif ph == "l1_swa":
                    emit_swa()
            prev_ph = ph
        yv = yT[s].rearrange("(c p) t -> p c t", p=128)
        for tt in range(NTT):
            k.dma("sp", yv[:, :, tsl(tt)], x_sb[:, :, tsl(tt)], reads=[x_t[c][tt] for c in range(NCH)], writes=[y_t])
    k.eng["sp"].wait_ge(y_t.dsem, y_t.dcnt)
    return nc, es


def host_consts():
    p = np.arange(128)
    ones = np.ones((128, 128), np.float32)
    U = (p[:, None] <= p[None, :]).astype(np.float32)
    r = np.arange(128)
    dist = np.concatenate([128 + r[None, :] - p[:, None], r[None, :] - p[:, None]], axis=1)
    valid = ((dist >= 0) & (dist < 128)).astype(np.float32)
    distc = np.clip(dist, 0, 127).astype(np.float32)
    return ones, U, distc, valid


def make_in_maps(inputs, phases, n_seq_per_core, n_cores):
    ones, U, distc, valid = host_consts()
    x = np.asarray(inputs["x"], np.float32)
    gnames = ["l0_ffn1_norm", "l0_mix_norm", "l0_ffn2_norm", "l1_ffn1_norm", "l1_mix_norm", "l1_ffn2_norm", "final_norm"]
    gains = np.stack([np.asarray(inputs[n], np.float32).reshape(NCH, 128).T for n in gnames], axis=1)
    gains = np.ascontiguousarray(gains.reshape(128, 7 * NCH))
    pidx = np.arange(128)
    ident = np.eye(128, dtype=np.float32)
    amask = np.where(pidx[:, None] <= pidx[None, :], 0.0, -16384.0).astype(np.float32)
    shared = {"gains": gains, "c_ones": ones, "c_U": U, "c_tri": U, "c_ident": ident, "c_amask": amask}
    for ph in phases:
        if "ffn" in ph:
            shared[ph + "_wg"] = np.ascontiguousarray(inputs[ph + "_w_gate"], np.float32)
            shared[ph + "_wu"] = np.ascontiguousarray(inputs[ph + "_w_up"], np.float32)
            shared[ph + "_wd"] = np.ascontiguousarray(inputs[ph + "_w_down"], np.float32)
        elif ph == "l0_fox":
            shared["fox_win"] = np.ascontiguousarray(inputs["l0_fox_w_in"], np.float32)
            shared["fox_bfb"] = np.ascontiguousarray(np.broadcast_to(np.asarray(inputs["l0_fox_b_forget"], np.float32)[None, :], (128, NH)))
            shared["fox_wout"] = np.ascontiguousarray(inputs["l0_fox_w_out"], np.float32)
            shared["fox_bcol"] = np.ascontiguousarray(np.asarray(inputs["l0_fox_b_forget"], np.float32).reshape(NH, 1))
        elif ph == "l1_swa":
            shared["swa_win"] = np.ascontiguousarray(inputs["l1_swa_w_in"], np.float32)
            shared["swa_sinkb"] = np.ascontiguousarray(np.broadcast_to(np.asarray(inputs["l1_swa_sinks"], np.float32)[None, :], (128, NH)))
            shared["swa_wout"] = np.ascontiguousarray(inputs["l1_swa_w_out"], np.float32)
            shared["c_distc"] = distc
            shared["c_valid"] = valid
    in_maps = []
    for ci in range(n_cores):
        xs = x[ci * n_seq_per_core:(ci + 1) * n_seq_per_core]
        m = dict(shared)
        m["xT"] = np.ascontiguousarray(np.transpose(xs, (0, 2, 1)))
        in_maps.append(m)
    return in_maps


_CACHE = {}


def run_phases(inputs, phases, n_cores=8, n_seq_per_core=2, trace=False):
    key = (tuple(phases), n_seq_per_core)
    if key not in _CACHE:
        _CACHE[key] = build_program(phases, n_seq_per_core)
    nc, _es = _CACHE[key]
    in_maps = make_in_maps(inputs, phases, n_seq_per_core, n_cores)
    res = run_bass_kernel_spmd(nc, in_maps, core_ids=list(range(n_cores)), trace=trace)
    outs = [np.transpose(r["yT"], (0, 2, 1)) for r in res.results]
    return np.ascontiguousarray(np.concatenate(outs, axis=0)), res


def kernel(**inputs):
    out, _ = run_phases(inputs, ALL_PHASES)
    return out.astype(np.float32)
```

```python
import numpy as np
from contextlib import ExitStack
import concourse.bass as bass
import concourse.mybir as mybir
from concourse.bass_utils import run_bass_kernel_spmd

F32 = mybir.dt.float32
BF16 = mybir.dt.bfloat16
AF = mybir.ActivationFunctionType
ALU = mybir.AluOpType

D = 1024
S = 2048
DFF = 2816
NCH = 8
TT = 512
NTT = S // TT
NJ = S // 128
NH = 16
G = 2
NGRP = DFF // (128 * G)
NBUF = 3
REG = 2048
SLOT = 3 * REG
EPS = 1e-6
ALL_PHASES = ["l0_ffn1", "l0_fox", "l0_ffn2", "l1_ffn1", "l1_swa", "l1_ffn2", "final"]
NORM_IDX = {"l0_ffn1": 0, "l0_fox": 1, "l0_ffn2": 2, "l1_ffn1": 3, "l1_swa": 4, "l1_ffn2": 5, "final": 6}
SLOPES = [float(2.0 ** (-8.0 * (h + 1) / NH)) for h in range(NH)]


class T:
    __slots__ = ("name", "lw", "rd", "dsem", "dkey", "dcnt")

    def __init__(self, name):
        self.name = name
        self.lw = None
        self.rd = {}
        self.dsem = None
        self.dkey = None
        self.dcnt = 0


class KB:
    def __init__(self, nc, es):
        self.nc = nc
        self.es = es
        self.eng = {"pe": nc.tensor, "act": nc.scalar, "dve": nc.vector, "pool": nc.gpsimd, "sp": nc.sync}
        self.sem = {k: es.enter_context(nc.semaphore("s_" + k)) for k in self.eng}
        self.cnt = {k: 0 for k in self.eng}
        self.waited = {k: {} for k in self.eng}
        self.nsem = 0

    def _need(self, e, reads, writes, skipkey=None):
        need = {}

        def add(st):
            if st is None:
                return
            k, sem, v = st
            if k == skipkey:
                return
            if e == "pe" and k == "pe":
                return
            if k not in need or need[k][1] < v:
                need[k] = (sem, v)

        for t in reads:
            add(t.lw)
        for t in writes:
            add(t.lw)
            for st in t.rd.values():
                add(st)
        w = self.waited[e]
        for k, (sem, v) in need.items():
            if w.get(k, 0) < v:
                self.eng[e].wait_ge(sem, v)
                w[k] = v

    def op(self, e, fn, reads=(), writes=(), inc=True):
        self._need(e, reads, writes)
        ins = fn(self.eng[e])
        if inc:
            self.cnt[e] += 1
            ins.then_inc(self.sem[e], 1)
            st = (e, self.sem[e], self.cnt[e])
        else:
            st = (e, self.sem[e], self.cnt[e] + 1)
        for t in writes:
            t.lw = st
            t.rd = {}
        for t in reads:
            t.rd[e] = st

    def fence(self):
        for e in ("pe", "act", "dve", "sp", "pool"):
            for o in ("pe", "act", "dve", "pool"):
                if e == o or self.cnt[o] == 0:
                    continue
                if self.waited[e].get(o, 0) < self.cnt[o]:
                    self.eng[e].wait_ge(self.sem[o], self.cnt[o])
                    self.waited[e][o] = self.cnt[o]

    def dma(self, q, out, in_, reads=(), writes=()):
        t = writes[0]
        if t.dsem is None:
            t.dkey = "d%d" % self.nsem
            t.dsem = self.es.enter_context(self.nc.semaphore(t.dkey))
            self.nsem += 1
        self._need(q, reads, writes, skipkey=t.dkey)
        ins = self.eng[q].dma_start(out=out, in_=in_)
        t.dcnt += 16
        ins.then_inc(t.dsem, 16)
        st = (t.dkey, t.dsem, t.dcnt)
        for w in writes:
            w.lw = st
            w.rd = {}
        for r in reads:
            r.rd[t.dkey] = st


def bcast(ap, pos, count):
    dims = [list(d) for d in ap.ap]
    dims.insert(1 + pos, [0, count])
    return bass.AP(ap.tensor, ap.offset, dims)


def weight_tasks(phases, n_seq):
    tasks = []
    for s in range(n_seq):
        for ph in phases:
            if ph.endswith("ffn1") or ph.endswith("ffn2"):
                for g in range(NGRP):
                    tasks.append(("ffn", ph, g))
            elif ph == "l0_fox":
                tasks.append(("fox_f", ph, 0))
                for hp in range(8):
                    tasks.append(("fox_pair", ph, hp))
            elif ph == "l1_swa":
                tasks.append(("swa_kv", ph, 0))
                for tg in range(NTT):
                    for hp in range(8):
                        tasks.append(("swa_q", ph, hp))
    return tasks


def build_program(phases, n_seq):
    nc = bass.Bass("TRN2", target_bir_lowering=False)
    es = ExitStack()
    k = KB(nc, es)
    dr = {}

    def din(name, shape):
        dr[name] = nc.dram_tensor(name, list(shape), F32, kind="ExternalInput").ap()
        return dr[name]

    xT = din("xT", [n_seq, D, S])
    gains = din("gains", [128, 7 * NCH])
    c_ones = din("c_ones", [128, 128])
    c_U = din("c_U", [128, 128])
    c_tri = din("c_tri", [128, 128])
    c_ident = din("c_ident", [128, 128])
    c_amask = din("c_amask", [128, 128])
    for ph in phases:
        if "ffn" in ph:
            din(ph + "_wg", [D, DFF])
            din(ph + "_wu", [D, DFF])
            din(ph + "_wd", [DFF, D])
        elif ph == "l0_fox":
            din("fox_win", [D, 3 * D + NH])
            din("fox_bfb", [128, NH])
            din("fox_bcol", [NH, 1])
            din("fox_wout", [D, D])
        elif ph == "l1_swa":
            din("swa_win", [D, D + 256])
            din("swa_sinkb", [128, NH])
            din("swa_wout", [D, D])
            din("c_distc", [128, 256])
            din("c_valid", [128, 256])
    yT = nc.dram_tensor("yT", [n_seq, D, S], F32, kind="ExternalOutput").ap()
    y_t = T("yT")

    def sb(name, shape, dt):
        return es.enter_context(nc.sbuf_tensor(name, list(shape), dt))

    x_sb = sb("x_sb", [128, NCH, S], F32)
    h_sb = sb("h_sb", [128, NCH, S], BF16)
    g_sb = sb("g_sb", [128, 7 * NCH], F32)
    ones_sb = sb("ones_sb", [128, 128], F32)
    U_sb = sb("U_sb", [128, 128], F32)
    tri_sb = sb("tri_sb", [128, 128], BF16)
    onesb_sb = sb("onesb_sb", [128, 128], BF16)
    ident_sb = sb("ident_sb", [128, 128], BF16)
    amask_sb = sb("amask_sb", [128, 128], BF16)
    ring = sb("ring", [128, NBUF * SLOT], BF16)
    ARENA = 72704
    arena = sb("arena", [128, ARENA // 4], F32)
    banks = [es.enter_context(nc.psum_tensor("P%d" % i, [128, 512], F32)) for i in range(8)]
    bank_t = [T("P%d" % i) for i in range(8)]

    def av(off, nbytes, dt):
        assert off % 4 == 0 and off + nbytes <= ARENA
        a = arena[:, off // 4:(off + nbytes) // 4]
        if dt == BF16:
            a = a.bitcast(BF16) if hasattr(a, "bitcast") else None
        return a

    x_t = [[T("x%d_%d" % (c, tt)) for tt in range(NTT)] for c in range(NCH)]
    h_t = [T("h%d" % tt) for tt in range(NTT)]
    g_t = T("g")
    const_t = T("consts")
    tri_t = T("tri")
    onesb_t = T("onesb")
    cmask_t = T("cmask")

    k.dma("sp", g_sb[:], gains, writes=[g_t])
    k.dma("sp", ones_sb[:], c_ones, writes=[const_t])
    k.dma("sp", U_sb[:], c_U, writes=[const_t])
    k.dma("pool", tri_sb[:], c_tri, writes=[tri_t])
    k.dma("pool", onesb_sb[:], c_ones, writes=[onesb_t])
    k.dma("pool", ident_sb[:], c_ident, writes=[cmask_t])
    k.dma("pool", amask_sb[:], c_amask, writes=[cmask_t])

    tasks = weight_tasks(phases, n_seq)
    reg_t = [[T("r%d_%d" % (s_, r)) for r in range(3)] for s_ in range(NBUF)]
    wstate = {"loaded": 0, "cur": -1}

    def rview(slot, r, off, n):
        base = slot * SLOT + r * REG + off
        return ring[:, base:base + n]

    def emit_load(i):
        kind, ph, idx = tasks[i]
        slot = i % NBUF
        rt = reg_t[slot]
        if kind == "ffn":
            f0 = idx * G * 128
            wg = dr[ph + "_wg"].rearrange("(c p) n -> p c n", p=128)[:, :, f0:f0 + G * 128]
            wu = dr[ph + "_wu"].rearrange("(c p) n -> p c n", p=128)[:, :, f0:f0 + G * 128]
            wd = dr[ph + "_wd"].rearrange("(g p) n -> p g n", p=128)[:, idx * G:(idx + 1) * G, :]
            k.dma("pool", rview(slot, 0, 0, REG).rearrange("p (c n) -> p c n", c=NCH), wg, writes=[rt[0]])
            k.dma("pool", rview(slot, 1, 0, REG).rearrange("p (c n) -> p c n", c=NCH), wu, writes=[rt[1]])
            k.dma("pool", rview(slot, 2, 0, REG).rearrange("p (g n) -> p g n", g=G), wd, writes=[rt[2]])
        elif kind == "fox_f":
            w = dr["fox_win"].rearrange("(c p) n -> p c n", p=128)[:, :, 3 * D:3 * D + NH]
            k.dma("pool", rview(slot, 0, 0, NCH * NH).rearrange("p (c n) -> p c n", c=NCH), w, writes=[rt[0]])
        elif kind == "fox_pair":
            w = dr["fox_win"].rearrange("(c p) n -> p c n", p=128)
            c0 = idx * 128
            k.dma("pool", rview(slot, 0, 0, 1024).rearrange("p (c n) -> p c n", c=NCH), w[:, :, c0:c0 + 128], writes=[rt[0]])
            k.dma("pool", rview(slot, 0, 1024, 1024).rearrange("p (c n) -> p c n", c=NCH), w[:, :, D + c0:D + c0 + 128], writes=[rt[0]])
            k.dma("pool", rview(slot, 1, 0, 1024).rearrange("p (c n) -> p c n", c=NCH), w[:, :, 2 * D + c0:2 * D + c0 + 128], writes=[rt[1]])
            k.dma("pool", rview(slot, 1, 1024, 1024), dr["fox_wout"][c0:c0 + 128, :], writes=[rt[1]])
        elif kind == "swa_kv":
            w = dr["swa_win"].rearrange("(c p) n -> p c n", p=128)
            dst = rview(slot, 0, 0, REG).rearrange("p (c k n) -> p c k n", c=NCH, k=2)
            for kvh in range(2):
                for dup in range(2):
                    k.dma("pool", dst[:, :, kvh, dup * 64:dup * 64 + 64], w[:, :, D + kvh * 64:D + kvh * 64 + 64], writes=[rt[0]])
            k.dma("pool", rview(slot, 1, 0, 1024).rearrange("p (c n) -> p c n", c=NCH), w[:, :, D + 128:D + 256], writes=[rt[1]])
        elif kind == "swa_q":
            w = dr["swa_win"].rearrange("(c p) n -> p c n", p=128)
            c0 = idx * 128
            k.dma("pool", rview(slot, 0, 0, 1024).rearrange("p (c n) -> p c n", c=NCH), w[:, :, c0:c0 + 128], writes=[rt[0]])

    def wnext(kind):
        wstate["cur"] += 1
        i = wstate["cur"]
        assert tasks[i][0] == kind, (tasks[i], kind)
        while wstate["loaded"] <= min(i + 1, len(tasks) - 1):
            emit_load(wstate["loaded"])
            wstate["loaded"] += 1
        return i % NBUF

    def fview(off, n):
        return arena[:, off // 4: off // 4 + n]

    def bview(off, n):
        full = arena[:, off // 4: off // 4 + (n + 1) // 2]
        return full.bitcast(BF16)

    sq8 = bview(0, NCH * TT).rearrange("p (c n) -> p c n", c=NCH)
    sq8_t = T("sq8")
    rs = fview(8192, 512)
    rs_t = T("rs")
    rstd = [fview(18432, 512), fview(20480, 512)]
    rstd_t = [T("rstd0"), T("rstd1")]
    cnt = {"sq": 0, "rstd": 0, "gate": 0, "up": 0, "down": 0, "sg": 0, "act": 0, "proj": 0, "sbank": 0, "obank": 0, "pt": 0, "rd": 0}

    def rot(name, n):
        v = cnt[name] % n
        cnt[name] += 1
        return v

    def tsl(tt):
        return slice(tt * TT, (tt + 1) * TT)

    def emit_norm(nidx, final=False):
        for tt in range(NTT):
            ps, ps_t = banks[7], bank_t[7]
            k.op("act", lambda e: e.activation(out=sq8, in_=x_sb[:, :, tsl(tt)], func=AF.Square),
                 reads=[x_t[c][tt] for c in range(NCH)], writes=[sq8_t])
            for c in range(NCH):
                k.op("pe", lambda e: e.matmul(ps[:], onesb_sb[:], sq8[:, c, :], start=(c == 0), stop=(c == NCH - 1)),
                     reads=[sq8_t, onesb_t], writes=[ps_t], inc=(c == NCH - 1))
            k.op("act", lambda e: e.activation(out=rs, in_=ps[:], func=AF.Ln, scale=1.0 / D, bias=eps_sb[:, 0:1]),
                 reads=[ps_t, eps_t], writes=[rs_t])
            r = rot("rstd", 2)
            k.op("act", lambda e: e.activation(out=rstd[r], in_=rs, func=AF.Exp, scale=-0.5), reads=[rs_t], writes=[rstd_t[r]])
            for c in range(NCH):
                gcol = g_sb[:, nidx * NCH + c: nidx * NCH + c + 1]
                if final:
                    k.op("dve", lambda e: e.scalar_tensor_tensor(out=x_sb[:, c, tsl(tt)], in0=x_sb[:, c, tsl(tt)], scalar=gcol,
                                                                 in1=rstd[r], op0=ALU.mult, op1=ALU.mult),
                         reads=[rstd_t[r], g_t], writes=[x_t[c][tt]])
                else:
                    k.op("dve", lambda e: e.scalar_tensor_tensor(out=h_sb[:, c, tsl(tt)], in0=x_sb[:, c, tsl(tt)], scalar=gcol,
                                                                 in1=rstd[r], op0=ALU.mult, op1=ALU.mult),
                         reads=[x_t[c][tt], rstd_t[r], g_t], writes=[h_t[tt]])

    eps_sb = sb("eps_sb", [128, 1], F32)
    eps_t = T("eps")
    k.op("dve", lambda e: e.memset(eps_sb[:], EPS), writes=[eps_t])

    sg = [fview(10240, 512), fview(12288, 512)]
    sg_t = [T("sg0"), T("sg1")]
    actb = [bview(14336, G * 512).rearrange("p (g n) -> p g n", g=G), bview(14336 + G * 1024, G * 512).rearrange("p (g n) -> p g n", g=G)]
    act_t = [T("act0"), T("act1")]

    def emit_ffn(ph):
        pending = [None]
        for g in range(NGRP):
            slot = wnext("ffn")
            rt = reg_t[slot]
            wg = rview(slot, 0, 0, REG).rearrange("p (c n) -> p c n", c=NCH)
            wu = rview(slot, 1, 0, REG).rearrange("p (c n) -> p c n", c=NCH)
            wd = rview(slot, 2, 0, REG).rearrange("p (g n) -> p g n", g=G)
            for tt in range(NTT):
                a = rot("act", 2)
                for gi in range(G):
                    gb = rot("gate", 2)
                    ub = 2 + rot("up", 2)
                    for c in range(NCH):
                        k.op("pe", lambda e: e.matmul(banks[gb][:], wg[:, c, gi * 128:(gi + 1) * 128], h_sb[:, c, tsl(tt)],
                                                      start=(c == 0), stop=(c == NCH - 1)),
                             reads=[rt[0], h_t[tt]], writes=[bank_t[gb]], inc=(c == NCH - 1))
                    for c in range(NCH):
                        k.op("pe", lambda e: e.matmul(banks[ub][:], wu[:, c, gi * 128:(gi + 1) * 128], h_sb[:, c, tsl(tt)],
                                                      start=(c == 0), stop=(c == NCH - 1)),
                             reads=[rt[1], h_t[tt]], writes=[bank_t[ub]], inc=(c == NCH - 1))
                    si = rot("sg", 2)
                    k.op("act", lambda e: e.activation(out=sg[si], in_=banks[gb][:], func=AF.Silu),
                         reads=[bank_t[gb]], writes=[sg_t[si]])
                    k.op("dve", lambda e: e.tensor_tensor(out=actb[a][:, gi, :], in0=sg[si], in1=banks[ub][:], op=ALU.mult),
                         reads=[sg_t[si], bank_t[ub]], writes=[act_t[a]])
                if pending[0] is not None:
                    pending[0]()

                def down(a=a, tt=tt, wd=wd, rt=rt):
                    for dc in range(NCH):
                        db = 4 + rot("down", 4)
                        for gi in range(G):
                            k.op("pe", lambda e: e.matmul(banks[db][:], wd[:, gi, dc * 128:(dc + 1) * 128], actb[a][:, gi, :],
                                                          start=(gi == 0), stop=(gi == G - 1)),
                                 reads=[rt[2], act_t[a]], writes=[bank_t[db]], inc=(gi == G - 1))
                        k.op("dve", lambda e: e.scalar_tensor_tensor(out=x_sb[:, dc, tsl(tt)], in0=banks[db][:], scalar=0.5,
                                                                     in1=x_sb[:, dc, tsl(tt)], op0=ALU.mult, op1=ALU.add),
                             reads=[bank_t[db]], writes=[x_t[dc][tt]])
                pending[0] = down
        pending[0]()

    one_sb = sb("one_sb", [128, 1], F32)
    k.op("dve", lambda e: e.memset(one_sb[:], 1.0), writes=[eps_t])

    proj_banks = [[0, 1]]

    def projbank():
        pbk = proj_banks[0]
        i = pbk[rot("proj", len(pbk))]
        return banks[i], bank_t[i]

    class Pipe:
        def __init__(self, delay):
            self.q = []
            self.delay = delay

        def push(self, fn):
            self.q.append(fn)
            if len(self.q) > self.delay:
                self.q.pop(0)()

        def flush(self):
            while self.q:
                self.q.pop(0)()

    def emit_outproj(wo, wo_t, oT, oT_t):
        for dc in range(NCH):
            for tg in range(NTT):
                pb, pb_t = projbank()
                k.op("pe", lambda e: e.matmul(pb[:], wo[:, dc * 128:(dc + 1) * 128], oT[:, tsl(tg)], start=True, stop=True),
                     reads=[wo_t, oT_t[tg]], writes=[pb_t])
                k.op("dve", lambda e: e.tensor_tensor(out=x_sb[:, dc, tsl(tg)], in0=pb[:], in1=x_sb[:, dc, tsl(tg)], op=ALU.add),
                     reads=[pb_t], writes=[x_t[dc][tg]])

    def gen_outproj_tg(wo, wo_t, oT, oT_t, tg):
        for dc in range(NCH):
            pb, pb_t = projbank()
            k.op("pe", lambda e: e.matmul(pb[:], wo[:, dc * 128:(dc + 1) * 128], oT[:, tsl(tg)], start=True, stop=True),
                 reads=[wo_t, oT_t[tg]], writes=[pb_t])
            k.op("dve", lambda e: e.tensor_tensor(out=x_sb[:, dc, tsl(tg)], in0=pb[:], in1=x_sb[:, dc, tsl(tg)], op=ALU.add),
                 reads=[pb_t], writes=[x_t[dc][tg]])
            yield

    ucount = [0]

    def bg_step(bg):
        while bg:
            if len(bg[0]) > 2 and bg[0][2] > ucount[0]:
                return
            try:
                next(bg[0][1])
                return
            except StopIteration:
                bg.pop(0)

    def bg_drain(bg):
        while bg:
            drain(bg.pop(0)[1])

    def bg_has_older(bg, hp):
        return any(ent[0] < hp for ent in bg)

    def gen_projT(w, w_t, evac):
        for tg in range(NTT):
            pb, pb_t = projbank()
            for c in range(NCH):
                k.op("pe", lambda e: e.matmul(pb[:], w[:, c, :], h_sb[:, c, tsl(tg)], start=(c == 0), stop=(c == NCH - 1)),
                     reads=[w_t, h_t[tg]], writes=[pb_t], inc=(c == NCH - 1))
            evac(tg, pb, pb_t)
            yield

    def gen_projV(w, w_t, vaug, vaug_t):
        for jg in range(4):
            pb, pb_t = projbank()
            pv = pb[:].rearrange("p (j n) -> p j n", j=4)
            for jj in range(4):
                j = jg * 4 + jj
                for c in range(NCH):
                    k.op("pe", lambda e: e.matmul(pv[:, jj, :], h_sb[:, c, j * 128:(j + 1) * 128], w[:, c, :],
                                                  start=(c == 0), stop=(c == NCH - 1)),
                         reads=[w_t, h_t[j // 4]], writes=[pb_t], inc=(c == NCH - 1))
            k.op("dve", lambda e: e.tensor_copy(out=vaug[:, jg * 4:(jg + 1) * 4, :, 0:64],
                                                in_=pb[:].rearrange("p (j h d) -> p j h d", j=4, h=2)),
                 reads=[pb_t], writes=[vaug_t])
            yield

    def drain(gen):
        if gen is not None:
            for _ in gen:
                pass

    def emit_fox():
        pT = [bview(i * 1024, 512) for i in range(4)] + [bview(8192, 512), bview(9216, 512)]
        rd = [fview(4096, 512), fview(6144, 512)]
        A = fview(8192, 512)
        cpT = [fview(0, 512), fview(2048, 512)]
        qaug = [[bview(10240 + b_ * 16384 + hh_ * 4096, S) for hh_ in range(2)] for b_ in range(2)]
        kaug = [[bview(10240 + b_ * 16384 + 8192 + hh_ * 4096, S) for hh_ in range(2)] for b_ in range(2)]
        vaug = [bview(43008 + i * 8192, NJ * 2 * 128).rearrange("p (j h n) -> p j h n", j=NJ, h=2) for i in range(2)]
        oT = bview(59392, S)
        zel = fview(63488, NJ * NH).rearrange("p (j h) -> p j h", j=NJ)
        totp = fview(64512, (NJ + 1) * NH).rearrange("p (j h) -> p j h", j=NJ + 1)
        cp = fview(66560, NJ * NH).rearrange("p (j h) -> p j h", j=NJ)
        bfb = fview(67584, NH)
        bcol = fview(67648, 1)
        dq = bview(68608, S)
        qa_t = [[[T("qa%d_%d_%d" % (b_, hh_, i)) for i in range(NTT)] for hh_ in range(2)] for b_ in range(2)]
        ka_t = [[[T("ka%d_%d_%d" % (b_, hh_, i)) for i in range(NTT)] for hh_ in range(2)] for b_ in range(2)]
        qrow_t = [[T("qrow%d_%d" % (b_, hh_)) for hh_ in range(2)] for b_ in range(2)]
        augq_t = [[T("augq%d_%d" % (b_, hh_)) for hh_ in range(2)] for b_ in range(2)]
        augk_t = [[T("augk%d_%d" % (b_, hh_)) for hh_ in range(2)] for b_ in range(2)]
        oT_t = [T("oT%d" % i) for i in range(NTT)]
        vaug_t = [T("vaug0"), T("vaug1")]
        zel_t, totp_t, cp_t, bfb_t, A_t, dq_t = T("zel"), T("totp"), T("cp"), T("bfb"), T("A"), T("dq")
        cpT_t = [T("cpT0"), T("cpT1")]
        pT_t = [T("pT%d" % i) for i in range(6)]
        rd_t = [T("rd0"), T("rd1")]

        k.dma("sp", bfb, dr["fox_bfb"], writes=[bfb_t])
        k.dma("sp", bcol[0:NH, :], dr["fox_bcol"], writes=[bfb_t])
        slot = wnext("fox_f")
        wf = rview(slot, 0, 0, NCH * NH).rearrange("p (c n) -> p c n", c=NCH)
        wf_t = reg_t[slot][0]
        identf = fview(67712, NH)
        identf_t = T("identf")
        k.dma("sp", identf[0:NH, :], dr["c_ident"][0:NH, 0:NH], writes=[identf_t])
        cp_ps = banks[6][:, 0:NJ * NH].rearrange("p (j h) -> p j h", j=NJ)
        for tg in range(NTT):
            pb, pb_t = banks[tg], bank_t[tg]
            for c in range(NCH):
                k.op("pe", lambda e: e.matmul(pb[0:NH, :], wf[:, c, :], h_sb[:, c, tsl(tg)], start=(c == 0), stop=(c == NCH - 1)),
                     reads=[wf_t, h_t[tg]], writes=[pb_t], inc=(c == NCH - 1))
        A2 = [A, fview(63488, 512)]
        A2_t = [A_t, T("A1")]
        cpT4 = [fview(i * 2048, 512) for i in range(4)]
        cpT4_t = [T("cpT4_%d" % i) for i in range(4)]
        for tg in range(NTT):
            pb, pb_t = banks[tg], bank_t[tg]
            Aa, Aa_t = A2[tg % 2], A2_t[tg % 2]
            k.op("dve", lambda e: e.tensor_scalar(out=Aa[0:NH, :], in0=pb[0:NH, :], scalar1=bcol[0:NH, 0:1], scalar2=None, op0=ALU.add),
                 reads=[pb_t, bfb_t], writes=[Aa_t])
            k.op("act", lambda e: e.activation(out=Aa[0:NH, :], in_=Aa[0:NH, :], func=AF.Exp, scale=-1.0), reads=[Aa_t], writes=[Aa_t])
            k.op("act", lambda e: e.activation(out=Aa[0:NH, :], in_=Aa[0:NH, :], func=AF.Ln, scale=1.0, bias=one_sb[0:NH, 0:1]),
                 reads=[Aa_t, eps_t], writes=[Aa_t])
            init = 0.0 if tg == 0 else cpT4[tg - 1][0:NH, TT - 1:TT]
            rds = [Aa_t] + ([cpT4_t[tg - 1]] if tg > 0 else [])
            k.op("dve", lambda e: e.tensor_tensor_scan(out=cpT4[tg][0:NH, :], data0=Aa[0:NH, :], data1=Aa[0:NH, :], initial=init,
                                                       op0=ALU.add, op1=ALU.max),
                 reads=rds, writes=[cpT4_t[tg]])
            k.op("dve", lambda e: e.tensor_scalar(out=dq[0:NH, tsl(tg)], in0=cpT4[tg][0:NH, :], scalar1=-8.0, scalar2=None, op0=ALU.mult),
                 reads=[cpT4_t[tg]], writes=[dq_t])
        ONES2 = 1.0019378662109375
        one_b32 = bass.AP(ones_sb[64:128, 0:1].tensor, ones_sb[64:128, 0:1].offset, [list(ones_sb[64:128, 0:1].ap[0]), [0, S // 2]])
        one_row = bass.AP(ones_sb[64:65, 0:1].tensor, ones_sb[64:65, 0:1].offset, [list(ones_sb[64:65, 0:1].ap[0]), [0, S]])
        for b in range(2):
            v32 = fview(43008 + b * 8192, NJ * 2 * 64).rearrange("p (j h n) -> p j h n", j=NJ, h=2)
            k.op("dve", lambda e: e.memset(v32[:, :, :, 32:64], ONES2), writes=[vaug_t[b]])
            for hh in range(2):
                q32 = fview(10240 + b * 16384 + hh * 4096, S // 2)
                k32 = fview(10240 + b * 16384 + 8192 + hh * 4096, S // 2)
                k.op("dve", lambda e: e.memset(q32[64:128, :], 0.0), writes=[augq_t[b][hh]])
                k.op("act", lambda e: e.activation(out=k32[64:128, :], in_=one_b32, func=AF.Copy, scale=0.0),
                     reads=[const_t], writes=[augk_t[b][hh]])
                k.op("act", lambda e: e.activation(out=kaug[b][hh][64:65, :], in_=one_row, func=AF.Copy, scale=1.0),
                     reads=[const_t], writes=[augk_t[b][hh]])

        def emit_cp():
            for tg in range(NTT):
                for jj in range(4):
                    j = tg * 4 + jj
                    k.op("pe", lambda e: e.transpose(out=cp_ps[:, j, :], in_=cpT4[tg][0:NH, jj * 128:(jj + 1) * 128],
                                                     identity=identf[0:NH, :]),
                         reads=[cpT4_t[tg], identf_t], writes=[bank_t[6]])
            k.op("dve", lambda e: e.tensor_copy(out=cp, in_=cp_ps), reads=[bank_t[6]], writes=[cp_t])

        def start_proj(hp):
            slot = wnext("fox_pair")
            rt = reg_t[slot]
            b = hp % 2
            wq = rview(slot, 0, 0, 1024).rearrange("p (c n) -> p c n", c=NCH)
            wk = rview(slot, 0, 1024, 1024).rearrange("p (c n) -> p c n", c=NCH)
            wv = rview(slot, 1, 0, 1024).rearrange("p (c n) -> p c n", c=NCH)
            wo = rview(slot, 1, 1024, 1024)
            for hh in range(2):
                k.dma("sp", qaug[b][hh][64:65, :], dq[2 * hp + hh:2 * hp + hh + 1, :], reads=[dq_t, augq_t[b][hh]], writes=[qrow_t[b][hh]])

            def evac_to(dst, dst_t):
                def evac(tg, pb, pb_t):
                    for hh in range(2):
                        k.op("dve", lambda e: e.tensor_copy(out=dst[hh][0:64, tsl(tg)], in_=pb[hh * 64:hh * 64 + 64, :]),
                             reads=[pb_t], writes=[dst_t[hh][tg]])
                return evac

            def g():
                yield from gen_projT(wq, rt[0], evac_to(qaug[b], qa_t[b]))
                yield from gen_projT(wk, rt[0], evac_to(kaug[b], ka_t[b]))
                yield from gen_projV(wv, rt[1], vaug[b], vaug_t[b])
            return g(), (wo, rt[1])

        gen, wo_info = start_proj(0)
        drain(gen)
        gen = None
        emit_cp()
        proj_banks[0] = [0, 1, 7]
        NUNITS = 80
        NSTEPS = 16
        pipe = Pipe(3)
        bg = []
        for hp in range(8):
            b = hp % 2
            cur_wo = wo_info
            started = False
            credit = 0.0
            uidx = 0
            for hh in range(2):
                h = 2 * hp + hh
                rows = slice(hh * 64, hh * 64 + 64)
                for tg in range(NTT):
                    obi = 4 + rot("obank", 2)
                    ob, ob_t = banks[obi], bank_t[obi]
                    nkb = 4 * (tg + 1)
                    for kb in range(nkb):
                        c0 = max(0, kb * 128 - tg * TT)
                        sbi = (2, 3, 6)[rot("sbank", 3)]
                        sbk, sbk_t = banks[sbi], bank_t[sbi]
                        diag = kb >= 4 * tg
                        k.op("pe", lambda e: e.matmul(sbk[:, c0:TT], kaug[b][hh][:, kb * 128:(kb + 1) * 128],
                                                      qaug[b][hh][:, tg * TT + c0:(tg + 1) * TT], start=True, stop=not diag),
                             reads=[ka_t[b][hh][kb // 4], qa_t[b][hh][tg], qrow_t[b][hh], augq_t[b][hh], augk_t[b][hh]], writes=[sbk_t], inc=not diag)
                        if diag:
                            k.op("pe", lambda e: e.matmul(sbk[:, c0:c0 + 128], ident_sb[:], amask_sb[:], start=False, stop=True),
                                 reads=[cmask_t], writes=[sbk_t])
                        pi = rot("pt", 6)
                        k.op("act", lambda e: e.activation(out=pT[pi][:, c0:TT], in_=sbk[:, c0:TT], func=AF.Exp,
                                                           scale=0.125, bias=cp[:, kb, h:h + 1]),
                             reads=[sbk_t, cp_t], writes=[pT_t[pi]])

                        def pv(ob=ob, ob_t=ob_t, c0=c0, kb=kb, hh=hh, pi=pi, nkb=nkb, tg=tg, rows=rows, b=b, wo=cur_wo, hp=hp):
                            k.op("pe", lambda e: e.matmul(ob[:, c0:TT], vaug[b][:, kb, hh, :], pT[pi][:, c0:TT],
                                                          start=(kb == 0), stop=(kb == nkb - 1)),
                                 reads=[vaug_t[b], pT_t[pi]], writes=[ob_t])
                            if kb == nkb - 1:
                                ri = rot("rd", 2)
                                k.op("act", lambda e: e.activation(out=rd[ri][0:64, :], in_=ob[64:128, :], func=AF.Ln),
                                     reads=[ob_t], writes=[rd_t[ri]])
                                k.op("act", lambda e: e.activation(out=rd[ri][0:64, :], in_=rd[ri][0:64, :], func=AF.Exp, scale=-1.0),
                                     reads=[rd_t[ri]], writes=[rd_t[ri]])
                                k.op("dve", lambda e: e.tensor_tensor(out=oT[rows, tsl(tg)], in0=ob[0:64, :], in1=rd[ri][0:64, :],
                                                                      op=ALU.mult),
                                     reads=[ob_t, rd_t[ri]], writes=[oT_t[tg]])
                                if hh == 1:
                                    bg.append((hp, gen_outproj_tg(wo[0], wo[1], oT, oT_t, tg), ucount[0] + 7))
                        pipe.push(pv)
                        ucount[0] += 1
                        bg_step(bg)
                        uidx += 1
                        if (not started) and hp + 1 < 8 and uidx >= 4 and not bg_has_older(bg, hp):
                            gen, wo_info = start_proj(hp + 1)
                            started = True
                        if gen is not None:
                            credit += NSTEPS / (NUNITS - 14) + 0.02
                            while credit >= 1.0:
                                credit -= 1.0
                                next(gen, None)
            if hp + 1 < 8:
                assert started
            drain(gen)
            gen = None
        pipe.flush()
        bg_drain(bg)
        proj_banks[0] = [0, 1]

    def emit_swa():
        pT = [bview(i * 1024, 512).rearrange("p (s h n) -> p s h n", s=2, h=2) for i in range(3)]
        rd = [fview(4096, 512), fview(6144, 512)]
        ef = [fview(8192, 512).rearrange("p (s h n) -> p s h n", s=2, h=2),
              fview(10240, 512).rearrange("p (s h n) -> p s h n", s=2, h=2),
              fview(12288, 512).rearrange("p (s h n) -> p s h n", s=2, h=2)]
        qbd = [bview(14336 + b_ * 2048, 1024).rearrange("p (n h q) -> p n h q", n=4, h=2) for b_ in range(2)]
        kTd = bview(18432, 2 * S).rearrange("p (k t) -> p k t", k=2)
        vaug = bview(26624, NJ * 2 * 128).rearrange("p (j h n) -> p j h n", j=NJ, h=2)
        oT = bview(34816, 8 * TT).rearrange("p (g t) -> p g t", g=8)
        E = bview(43008, NH * 256).rearrange("p (h s n) -> p h s n", h=NH, s=2)
        wo_all = bview(51200, 8 * D).rearrange("p (g n) -> p g n", g=8)
        distc = fview(67584, 256)
        valid = fview(68608, 256)
        es_ = fview(69632, NH)
        sinkb = fview(69696, NH)
        qT_t = [T("sqT0"), T("sqT1")]
        qz_t = T("qz")
        kT_t = [[T("skT%d_%d" % (kv, i)) for i in range(NTT)] for kv in range(2)]
        oT_t = [T("soT0"), T("soT1")]
        wo_t = T("wo_all")
        vaug_t, E_t, cst_t, es_t = T("svaug"), T("E"), T("scst"), T("es")
        ef_t = [T("ef%d" % i) for i in range(3)]
        pT_t = [T("spT%d" % i) for i in range(3)]
        rd_t = [T("srd0"), T("srd1")]

        k.dma("sp", distc, dr["c_distc"], writes=[cst_t])
        k.dma("sp", valid, dr["c_valid"], writes=[cst_t])
        k.dma("sp", sinkb, dr["swa_sinkb"], writes=[cst_t])
        k.dma("pool", wo_all, dr["swa_wout"].rearrange("(g p) n -> p g n", p=128), writes=[wo_t])
        k.op("dve", lambda e: e.memset(vaug[:, :, :, 64:128], 1.0), writes=[vaug_t])
        for b_ in range(2):
            k.op("dve", lambda e: e.memset(qbd[b_][0:64, :, 1, :], 0.0), writes=[qz_t])
            k.op("dve", lambda e: e.memset(qbd[b_][64:128, :, 0, :], 0.0), writes=[qz_t])
        k.op("act", lambda e: e.activation(out=es_, in_=sinkb, func=AF.Exp), reads=[cst_t], writes=[es_t])
        ef0 = fview(8192, 256)
        ef1 = fview(10240, 256)
        for h in range(NH):
            i = h % 2
            efx = (ef0, ef1)[i]
            k.op("act", lambda e: e.activation(out=efx, in_=distc, func=AF.Exp, scale=-SLOPES[h]),
                 reads=[cst_t], writes=[ef_t[i]])
            k.op("dve", lambda e: e.tensor_tensor(out=E[:, h].rearrange("p s n -> p (s n)"), in0=efx, in1=valid, op=ALU.mult),
                 reads=[ef_t[i], cst_t], writes=[E_t])
        slot = wnext("swa_kv")
        rt = reg_t[slot]
        wkd = rview(slot, 0, 0, REG).rearrange("p (c k n) -> p c k n", c=NCH, k=2)
        wv = rview(slot, 1, 0, 1024).rearrange("p (c n) -> p c n", c=NCH)

        def evac_k(kvh):
            def evac(tg, pb, pb_t):
                k.op("dve", lambda e: e.tensor_copy(out=kTd[:, kvh, tsl(tg)], in_=pb[:]), reads=[pb_t], writes=[kT_t[kvh][tg]])
            return evac
        for kvh in range(2):
            drain(gen_projT(wkd[:, :, kvh, :], rt[0], evac_k(kvh)))
        drain(gen_projV(wv, rt[1], vaug, vaug_t))

        def qproj(tg, b):
            slot = wnext("swa_q")
            w_t = reg_t[slot][0]
            wq = rview(slot, 0, 0, 1024).rearrange("p (c n) -> p c n", c=NCH)
            pb, pb_t = projbank()
            for c in range(NCH):
                k.op("pe", lambda e: e.matmul(pb[:], wq[:, c, :], h_sb[:, c, tsl(tg)], start=(c == 0), stop=(c == NCH - 1)),
                     reads=[w_t, h_t[tg]], writes=[pb_t], inc=(c == NCH - 1))
            for hh in range(2):
                r = slice(hh * 64, hh * 64 + 64)
                if hh == 0:
                    k.op("act", lambda e: e.activation(out=qbd[b][r, :, hh, :], in_=pb[r, :].rearrange("p (n q) -> p n q", n=4),
                                                       func=AF.Copy),
                         reads=[pb_t, qz_t], writes=[qT_t[b]])
                else:
                    k.op("dve", lambda e: e.tensor_copy(out=qbd[b][r, :, hh, :], in_=pb[r, :].rearrange("p (n q) -> p n q", n=4)),
                         reads=[pb_t, qz_t], writes=[qT_t[b]])

        def gen_outproj_half(tg, half):
            for dc in range(NCH):
                pb, pb_t = projbank()
                for g in range(4 * half, 4 * half + 4):
                    k.op("pe", lambda e: e.matmul(pb[:], wo_all[:, g, dc * 128:(dc + 1) * 128], oT[:, g, :],
                                                  start=(g == 4 * half), stop=(g == 4 * half + 3)),
                         reads=[wo_t, oT_t[half]], writes=[pb_t], inc=(g == 4 * half + 3))
                k.op("dve", lambda e: e.tensor_tensor(out=x_sb[:, dc, tsl(tg)], in0=pb[:], in1=x_sb[:, dc, tsl(tg)], op=ALU.add),
                     reads=[pb_t], writes=[x_t[dc][tg]])
                yield

        def bg_drain_upto(bg, tag):
            while bg and bg[0][0] <= tag:
                drain(bg.pop(0)[1])

        pipe = Pipe(2)
        bg = []
        steps = [(tg, hp) for tg in range(NTT) for hp in range(8)]
        proj_banks[0] = [0, 1, 7]
        qproj(0, 0)
        for si, (tg, hp) in enumerate(steps):
            b = si % 2
            kvh = hp // 4
            Eperm = E[:, 2 * hp:2 * hp + 2].rearrange("p h s n -> p s h n")
            obs = []
            for hh in range(2):
                obi = (4, 5, 6)[rot("obank", 3)]
                obs.append((banks[obi], bank_t[obi]))
            for nn in range(4):
                n = 4 * tg + nn
                s0 = 0 if n > 0 else 1
                sbi = 2 + rot("sbank", 2)
                sbk, sbk_t = banks[sbi], bank_t[sbi]
                sv = sbk[:].rearrange("p (s h n) -> p s h n", s=2, h=2)
                for sl in range(s0, 2):
                    kb = n - 1 + sl
                    k.op("pe", lambda e: e.matmul(sv[:, sl], kTd[:, kvh, kb * 128:(kb + 1) * 128], qbd[b][:, nn],
                                                  start=True, stop=True),
                         reads=[kT_t[kvh][kb // 4], qT_t[b], qz_t], writes=[sbk_t], inc=(sl == 1))
                i = rot("pt", 3)
                k.op("act", lambda e: e.activation(out=ef[i][:, s0:2], in_=sv[:, s0:2], func=AF.Exp, scale=0.125),
                     reads=[sbk_t], writes=[ef_t[i]])
                k.op("dve", lambda e: e.tensor_tensor(out=pT[i][:, s0:2], in0=ef[i][:, s0:2], in1=Eperm[:, s0:2], op=ALU.mult),
                     reads=[ef_t[i], E_t], writes=[pT_t[i]])

                def pv(obs=obs, n=n, nn=nn, s0=s0, i=i, tg=tg, hp=hp, kvh=kvh):
                    for hh in range(2):
                        ob, ob_t = obs[hh]
                        for sl in range(s0, 2):
                            kb = n - 1 + sl
                            k.op("pe", lambda e: e.matmul(ob[:, nn * 128:(nn + 1) * 128], vaug[:, kb, kvh, :],
                                                          pT[i][:, sl, hh, :], start=(sl == s0), stop=(sl == 1)),
                                 reads=[vaug_t, pT_t[i]], writes=[ob_t], inc=(sl == 1))
                    if nn == 3:
                        half = hp // 4
                        bg_drain_upto(bg, (tg - 1) * 2 + half)
                        for hh in range(2):
                            ob, ob_t = obs[hh]
                            h = 2 * hp + hh
                            rows = slice(hh * 64, hh * 64 + 64)
                            ri = rot("rd", 2)
                            k.op("act", lambda e: e.activation(out=rd[ri][0:64, :], in_=ob[64:128, :], func=AF.Ln,
                                                               scale=1.0, bias=es_[0:64, h:h + 1]),
                                 reads=[ob_t, es_t], writes=[rd_t[ri]])
                            k.op("act", lambda e: e.activation(out=rd[ri][0:64, :], in_=rd[ri][0:64, :], func=AF.Exp, scale=-1.0),
                                 reads=[rd_t[ri]], writes=[rd_t[ri]])
                            k.op("dve", lambda e: e.tensor_tensor(out=oT[rows, hp, :], in0=ob[0:64, :], in1=rd[ri][0:64, :],
                                                                  op=ALU.mult),
                                 reads=[ob_t, rd_t[ri]], writes=[oT_t[half]])
                        if hp % 4 == 3:
                            bg.append((tg * 2 + half, gen_outproj_half(tg, half), ucount[0] + 3))
                pipe.push(pv)
                ucount[0] += 1
                bg_step(bg)
                if nn == 1 and si + 1 < len(steps):
                    qproj(steps[si + 1][0], 1 - b)
        pipe.flush()
        bg_drain(bg)
        proj_banks[0] = [0, 1]

    for s in range(n_seq):
        xv = xT[s].rearrange("(c p) t -> p c t", p=128)
        for tt in range(NTT):
            k.dma("sp", x_sb[:, :, tsl(tt)], xv[:, :, tsl(tt)], writes=[x_t[c][tt] for c in range(NCH)])
        prev_ph = None
        for ph in phases:
            if ph == "final":
                emit_norm(NORM_IDX[ph], final=True)
            else:
                if "ffn" in ph:
                    if not (prev_ph is not None and "ffn" in prev_ph):
                        k.fence()
                    emit_norm(NORM_IDX[ph])
                else:
                    emit_norm(NORM_IDX[ph])
                    k.fence()
                if "ffn" in ph:
                    emit_ffn(ph)
                elif ph == "l0_fox":
                    emit_fox()
                elif ph == "l1_swa":
                    emit_swa()
            prev_ph = ph
        yv = yT[s].rearrange("(c p) t -> p c t", p=128)
        for tt in range(NTT):
            k.dma("sp", yv[:, :, tsl(tt)], x_sb[:, :, tsl(tt)], reads=[x_t[c][tt] for c in range(NCH)], writes=[y_t])
    k.eng["sp"].wait_ge(y_t.dsem, y_t.dcnt)
    return nc, es


def host_consts():
    p = np.arange(128)
    ones = np.ones((128, 128), np.float32)
    U = (p[:, None] <= p[None, :]).astype(np.float32)
    r = np.arange(128)
    dist = np.concatenate([128 + r[None, :] - p[:, None], r[None, :] - p[:, None]], axis=1)
    valid = ((dist >= 0) & (dist < 128)).astype(np.float32)
    distc = np.clip(dist, 0, 127).astype(np.float32)
    return ones, U, distc, valid


def make_in_maps(inputs, phases, n_seq_per_core, n_cores):
    ones, U, distc, valid = host_consts()
    x = np.asarray(inputs["x"], np.float32)
    gnames = ["l0_ffn1_norm", "l0_mix_norm", "l0_ffn2_norm", "l1_ffn1_norm", "l1_mix_norm", "l1_ffn2_norm", "final_norm"]
    gains = np.stack([np.asarray(inputs[n], np.float32).reshape(NCH, 128).T for n in gnames], axis=1)
    gains = np.ascontiguousarray(gains.reshape(128, 7 * NCH))
    pidx = np.arange(128)
    ident = np.eye(128, dtype=np.float32)
    amask = np.where(pidx[:, None] <= pidx[None, :], 0.0, -16384.0).astype(np.float32)
    shared = {"gains": gains, "c_ones": ones, "c_U": U, "c_tri": U, "c_ident": ident, "c_amask": amask}
    for ph in phases:
        if "ffn" in ph:
            shared[ph + "_wg"] = np.ascontiguousarray(inputs[ph + "_w_gate"], np.float32)
            shared[ph + "_wu"] = np.ascontiguousarray(inputs[ph + "_w_up"], np.float32)
            shared[ph + "_wd"] = np.ascontiguousarray(inputs[ph + "_w_down"], np.float32)
        elif ph == "l0_fox":
            shared["fox_win"] = np.ascontiguousarray(inputs["l0_fox_w_in"], np.float32)
            shared["fox_bfb"] = np.ascontiguousarray(np.broadcast_to(np.asarray(inputs["l0_fox_b_forget"], np.float32)[None, :], (128, NH)))
            shared["fox_wout"] = np.ascontiguousarray(inputs["l0_fox_w_out"], np.float32)
            shared["fox_bcol"] = np.ascontiguousarray(np.asarray(inputs["l0_fox_b_forget"], np.float32).reshape(NH, 1))
        elif ph == "l1_swa":
            shared["swa_win"] = np.ascontiguousarray(inputs["l1_swa_w_in"], np.float32)
            shared["swa_sinkb"] = np.ascontiguousarray(np.broadcast_to(np.asarray(inputs["l1_swa_sinks"], np.float32)[None, :], (128, NH)))
            shared["swa_wout"] = np.ascontiguousarray(inputs["l1_swa_w_out"], np.float32)
            shared["c_distc"] = distc
            shared["c_valid"] = valid
    in_maps = []
    for ci in range(n_cores):
        xs = x[ci * n_seq_per_core:(ci + 1) * n_seq_per_core]
        m = dict(shared)
        m["xT"] = np.ascontiguousarray(np.transpose(xs, (0, 2, 1)))
        in_maps.append(m)
    return in_maps


_CACHE = {}


def run_phases(inputs, phases, n_cores=8, n_seq_per_core=2, trace=False):
    key = (tuple(phases), n_seq_per_core)
    if key not in _CACHE:
        _CACHE[key] = build_program(phases, n_seq_per_core)
    nc, _es = _CACHE[key]
    in_maps = make_in_maps(inputs, phases, n_seq_per_core, n_cores)
    res = run_bass_kernel_spmd(nc, in_maps, core_ids=list(range(n_cores)), trace=trace)
    outs = [np.transpose(r["yT"], (0, 2, 1)) for r in res.results]
    return np.ascontiguousarray(np.concatenate(outs, axis=0)), res


def kernel(**inputs):
    out, _ = run_phases(inputs, ALL_PHASES)
    return out.astype(np.float32)
```
